# Optimizing a Trainium2 kernel written in Bass

```python
import math
import jax, jax.numpy as jnp
from jax import lax
import numpy as np

D_MODEL = 1024
BATCH = 16
SEQ = 256
DEPTH = 2
DEC_BATCH = 4
DEC_SEQ = 2048
PAST_LEN = 256

GRID_W = 64
HEAD_DIM = 64
N_HEADS_A = 8
N_KV_A = 2
REP_A = N_HEADS_A // N_KV_A
N_HEADS_C = 4
SSM_WIDTH = 256
SSM_GROUP = 16
SSM_GROUPS = SSM_WIDTH // SSM_GROUP
SSM_STATE = 64
NA_WIN_R = 8
NA_WIN_C = 16
Q_BLOCK = 128
D_FF = -(-8 * D_MODEL // (3 * 256)) * 256
ROPE_THETA = 10000.0
ROT_HALF = HEAD_DIM // 2
ROT_FREQS = ROT_HALF // 2
WIDTH_A = N_HEADS_A * HEAD_DIM
KV_WIDTH_A = N_KV_A * HEAD_DIM
WIDTH_C = N_HEADS_C * HEAD_DIM
N_BRANCH = 3
IN_WIDTH = WIDTH_A + 2 * KV_WIDTH_A + SSM_WIDTH + 3 * WIDTH_C + N_BRANCH * D_MODEL
EPS = 1e-6

kernel_name = "hybrid_flow_backbone_ctx_prefix_step"

F32 = jnp.float32


def rms_norm(x, g):
    x32 = x.astype(F32)
    y = x32 * lax.rsqrt(jnp.mean(x32 * x32, axis=-1, keepdims=True) + EPS) * g.astype(F32)
    return y.astype(x.dtype)


def adaln(cvec, lp):
    m = jax.nn.silu(cvec) @ lp["w_mod"] + lp["b_mod"]
    return jnp.split(m, 6, axis=-1)


def modulate(h, shift, scale):
    return h * (1 + scale[:, None, :]) + shift[:, None, :]


def axial_rope_tables(L):
    t = jnp.arange(L)
    row = (t // GRID_W).astype(F32)
    col = (t % GRID_W).astype(F32)
    inv = 1.0 / (ROPE_THETA ** (jnp.arange(ROT_FREQS, dtype=F32) / ROT_FREQS))
    ar = row[:, None] * inv[None]
    ac = col[:, None] * inv[None]
    return jnp.cos(ar), jnp.sin(ar), jnp.cos(ac), jnp.sin(ac)


def _rot(xs, cos, sin):
    x1, x2 = xs[..., :ROT_FREQS], xs[..., ROT_FREQS:]
    return jnp.concatenate([x1 * cos - x2 * sin, x2 * cos + x1 * sin], axis=-1)


def apply_axial_rope(x, tables):
    cr, sr, cc, sc = tables
    shp = (x.shape[1],) + (1,) * (x.ndim - 3) + (ROT_FREQS,)
    cr, sr, cc, sc = (a.reshape(shp) for a in (cr, sr, cc, sc))
    x32 = x.astype(F32)
    out = jnp.concatenate([_rot(x32[..., :ROT_HALF], cr, sr), _rot(x32[..., ROT_HALF:], cc, sc)], axis=-1)
    return out.astype(x.dtype)


def blocked_attention(q, k, v):
    bsz, lq, g, r, hd = q.shape
    nb = lq // Q_BLOCK
    qb = jnp.moveaxis(q.reshape(bsz, nb, Q_BLOCK, g, r, hd), 1, 0)
    scale = HEAD_DIM ** -0.5

    def block(qi):
        s = jnp.einsum('bqgrd,bkgd->bgrqk', qi, k).astype(F32) * scale
        p = jax.nn.softmax(s, axis=-1).astype(v.dtype)
        return jnp.einsum('bgrqk,bkgd->bqgrd', p, v)

    o = lax.map(block, qb)
    return jnp.moveaxis(o, 0, 1).reshape(bsz, lq, g, r, hd)


def neighborhood_attention(q, k, v, ck, cv, bias_table):
    bsz, L, H, hd = q.shape
    rows = L // GRID_W
    wr = min(NA_WIN_R, rows)
    wc = NA_WIN_C
    nw = wr * wc
    t = jnp.arange(L)
    r = t // GRID_W
    col = t % GRID_W
    rs = jnp.clip(r - wr // 2, 0, rows - wr)
    cs = jnp.clip(col - wc // 2, 0, GRID_W - wc)
    kr = rs[:, None] + jnp.arange(wr)[None]
    kc = cs[:, None] + jnp.arange(wc)[None]
    idx = (kr[:, :, None] * GRID_W + kc[:, None, :]).reshape(L, nw)
    dr = kr - r[:, None] + (NA_WIN_R - 1)
    dc = kc - col[:, None] + (NA_WIN_C - 1)
    bias = bias_table[:, dr[:, :, None], dc[:, None, :]].reshape(H, L, nw)
    nb = L // Q_BLOCK
    qb = jnp.moveaxis(q.reshape(bsz, nb, Q_BLOCK, H, hd), 1, 0)
    ib = idx.reshape(nb, Q_BLOCK, nw)
    bb = jnp.moveaxis(bias.reshape(H, nb, Q_BLOCK, nw), 1, 0)
    scale = HEAD_DIM ** -0.5

    def block(args):
        qi, ii, bi = args
        kg = jnp.take(k, ii, axis=1)
        vg = jnp.take(v, ii, axis=1)
        s_loc = jnp.einsum('bqhd,bqwhd->bhqw', qi, kg).astype(F32) * scale + bi[None].astype(F32)
        s_ctx = jnp.einsum('bqhd,bkhd->bhqk', qi, ck).astype(F32) * scale
        p = jax.nn.softmax(jnp.concatenate([s_loc, s_ctx], axis=-1), axis=-1).astype(v.dtype)
        return (jnp.einsum('bhqw,bqwhd->bqhd', p[..., :nw], vg)
                + jnp.einsum('bhqk,bkhd->bqhd', p[..., nw:], cv))

    o = lax.map(block, (qb, ib, bb))
    return jnp.moveaxis(o, 0, 1).reshape(bsz, L, H, hd)


def _scan_combine(e1, e2):
    a1, b1 = e1
    a2, b2 = e2
    return a1 * a2, a2 * b1 + b2


def ssm_scan(u32, lam_re, lam_im, log_step, b_re, b_im, c_re, c_im, h0, reverse):
    step = jnp.exp(log_step.astype(F32))
    lam = lax.complex(lam_re.astype(F32), lam_im.astype(F32))
    lam_bar = jnp.exp(lam * step[:, None])
    b = lax.complex(b_re.astype(F32), b_im.astype(F32))
    b_bar = ((lam_bar - 1) / lam)[..., None] * b
    cmat = lax.complex(c_re.astype(F32), c_im.astype(F32))
    bu = jnp.einsum('blgc,gpc->blgp', u32.astype(jnp.complex64), b_bar)
    if reverse:
        bu = jnp.flip(bu, axis=1)
    bu = bu.at[:, 0].add(lam_bar[None] * h0)
    a = jnp.broadcast_to(lam_bar, bu.shape)
    _, h = lax.associative_scan(_scan_combine, (a, bu), axis=1)
    h_final = h[:, -1]
    if reverse:
        h = jnp.flip(h, axis=1)
    y = jnp.real(jnp.einsum('blgp,gcp->blgc', h, cmat))
    return y, h_final


def s5_mixer(u, lp, state):
    bsz, L, _ = u.shape
    u32 = u.astype(F32).reshape(bsz, L, SSM_GROUPS, SSM_GROUP)
    st = state.astype(F32)
    ys, finals = [], []
    for d in range(2):
        h0 = lax.complex(st[:, d, 0], st[:, d, 1])
        y, hf = ssm_scan(u32, lp["ssm_lam_re"][d], lp["ssm_lam_im"][d], lp["ssm_log_step"][d],
                         lp["ssm_b_re"][d], lp["ssm_b_im"][d], lp["ssm_c_re"][d], lp["ssm_c_im"][d],
                         h0, reverse=(d == 1))
        ys.append(y)
        finals.append(jnp.stack([jnp.real(hf), jnp.imag(hf)], axis=1))
    y = (ys[0] + ys[1]).reshape(bsz, L, SSM_WIDTH) + lp["ssm_d"].astype(F32) * u32.reshape(bsz, L, SSM_WIDTH)
    y = jax.nn.gelu(y)
    y = y * jax.nn.sigmoid(y @ lp["ssm_w_glu"].astype(F32))
    return y.astype(u.dtype), jnp.stack(finals, axis=1)


def project_inputs(h, lp):
    bsz, L, _ = h.shape
    p = h @ lp["w_in"]
    sizes = [WIDTH_A, KV_WIDTH_A, KV_WIDTH_A, SSM_WIDTH, WIDTH_C, WIDTH_C, WIDTH_C]
    offs = [int(o) for o in np.cumsum(sizes)]
    qa, ka, va, u, qc, kc, vc, g = jnp.split(p, offs, axis=-1)
    qa = rms_norm(qa.reshape(bsz, L, N_KV_A, REP_A, HEAD_DIM), lp["qn_g"])
    ka = rms_norm(ka.reshape(bsz, L, N_KV_A, HEAD_DIM), lp["kn_g"])
    va = va.reshape(bsz, L, N_KV_A, HEAD_DIM)
    qc = qc.reshape(bsz, L, N_HEADS_C, HEAD_DIM)
    kc = kc.reshape(bsz, L, N_HEADS_C, HEAD_DIM)
    vc = vc.reshape(bsz, L, N_HEADS_C, HEAD_DIM)
    g = g.reshape(bsz, L, N_BRANCH, D_MODEL)
    return qa, ka, va, u, qc, kc, vc, g


def merge_branches(oa, ob, oc, g, lp):
    bsz, L = oa.shape[:2]
    gs = jax.nn.sigmoid(g)
    merged = (gs[:, :, 0] * (oa.reshape(bsz, L, WIDTH_A) @ lp["w_br_a"])
              + gs[:, :, 1] * (ob @ lp["w_br_b"])
              + gs[:, :, 2] * (oc.reshape(bsz, L, WIDTH_C) @ lp["w_br_c"]))
    return merged @ lp["w_out"]


def swiglu(h, lp):
    gu = h @ lp["w_ffn_gu"]
    gt, up = jnp.split(gu, 2, axis=-1)
    return (jax.nn.silu(gt) * up) @ lp["w_ffn_d"]


def trunk_layer(x, cvec, lp, mix_fn):
    sh1, sc1, gt1, sh2, sc2, gt2 = adaln(cvec, lp)
    h = modulate(rms_norm(x, lp["norm1_g"]), sh1, sc1)
    qa, ka, va, u, qc, kc, vc, g = project_inputs(h, lp)
    oa, ob, oc, aux = mix_fn(qa, ka, va, u, qc, kc, vc)
    x = x + gt1[:, None, :] * merge_branches(oa, ob, oc, g, lp)
    h2 = modulate(rms_norm(x, lp["norm2_g"]), sh2, sc2)
    x = x + gt2[:, None, :] * swiglu(h2, lp)
    return x, aux


def setup_inputs(seed: int = 0) -> dict:
    key = jax.random.key(seed)
    ks = iter(jax.random.split(key, 48))
    nrm = lambda shape, s=1.0: jax.random.normal(next(ks), shape, F32) * s
    D = D_MODEL
    n_idx = jnp.arange(SSM_STATE, dtype=F32)
    inp = {}
    inp["x_prompt"] = nrm((BATCH, SEQ, D))
    inp["x_sample"] = nrm((DEC_BATCH, DEC_SEQ, D))
    inp["c"] = nrm((DEC_BATCH, D))
    inp["cache_ga_k"] = nrm((DEC_BATCH, DEPTH, PAST_LEN, N_KV_A, HEAD_DIM))
    inp["cache_ga_v"] = nrm((DEC_BATCH, DEPTH, PAST_LEN, N_KV_A, HEAD_DIM))
    inp["cache_na_k"] = nrm((DEC_BATCH, DEPTH, PAST_LEN, N_HEADS_C, HEAD_DIM))
    inp["cache_na_v"] = nrm((DEC_BATCH, DEPTH, PAST_LEN, N_HEADS_C, HEAD_DIM))
    inp["state_ssm"] = nrm((DEC_BATCH, DEPTH, 2, 2, SSM_GROUPS, SSM_STATE), 0.1)
    inp["c_ctx"] = nrm((D,))
    inp["w_mod"] = nrm((DEPTH, D, 6 * D), 0.5 * D ** -0.5)
    inp["b_mod"] = nrm((DEPTH, 6 * D), 0.02)
    inp["norm1_g"] = 1.0 + nrm((DEPTH, D), 0.01)
    inp["w_in"] = nrm((DEPTH, D, IN_WIDTH), D ** -0.5)
    inp["qn_g"] = 1.0 + nrm((DEPTH, HEAD_DIM), 0.01)
    inp["kn_g"] = 1.0 + nrm((DEPTH, HEAD_DIM), 0.01)
    inp["ssm_lam_re"] = -0.5 + nrm((DEPTH, 2, SSM_GROUPS, SSM_STATE), 0.01)
    inp["ssm_lam_im"] = math.pi * n_idx + nrm((DEPTH, 2, SSM_GROUPS, SSM_STATE), 0.01)
    inp["ssm_log_step"] = jax.random.uniform(next(ks), (DEPTH, 2, SSM_GROUPS), F32,
                                             minval=math.log(1e-3), maxval=math.log(1e-1))
    inp["ssm_b_re"] = nrm((DEPTH, 2, SSM_GROUPS, SSM_STATE, SSM_GROUP), (2 * SSM_GROUP) ** -0.5)
    inp["ssm_b_im"] = nrm((DEPTH, 2, SSM_GROUPS, SSM_STATE, SSM_GROUP), (2 * SSM_GROUP) ** -0.5)
    inp["ssm_c_re"] = nrm((DEPTH, 2, SSM_GROUPS, SSM_GROUP, SSM_STATE), SSM_STATE ** -0.5)
    inp["ssm_c_im"] = nrm((DEPTH, 2, SSM_GROUPS, SSM_GROUP, SSM_STATE), SSM_STATE ** -0.5)
    inp["ssm_d"] = nrm((DEPTH, SSM_WIDTH))
    inp["ssm_w_glu"] = nrm((DEPTH, SSM_WIDTH, SSM_WIDTH), SSM_WIDTH ** -0.5)
    inp["na_bias"] = nrm((DEPTH, N_HEADS_C, 2 * NA_WIN_R - 1, 2 * NA_WIN_C - 1), 0.02)
    inp["w_br_a"] = nrm((DEPTH, WIDTH_A, D), WIDTH_A ** -0.5)
    inp["w_br_b"] = nrm((DEPTH, SSM_WIDTH, D), SSM_WIDTH ** -0.5)
    inp["w_br_c"] = nrm((DEPTH, WIDTH_C, D), WIDTH_C ** -0.5)
    inp["w_out"] = nrm((DEPTH, D, D), D ** -0.5)
    inp["norm2_g"] = 1.0 + nrm((DEPTH, D), 0.01)
    inp["w_ffn_gu"] = nrm((DEPTH, D, 2 * D_FF), D ** -0.5)
    inp["w_ffn_d"] = nrm((DEPTH, D_FF, D), D_FF ** -0.5)
    inp["final_g"] = 1.0 + nrm((D,), 0.01)
    return inp


def reference(x_prompt, x_sample, c, cache_ga_k, cache_ga_v, cache_na_k, cache_na_v, state_ssm, c_ctx,
              w_mod, b_mod, norm1_g, w_in, qn_g, kn_g, ssm_lam_re, ssm_lam_im, ssm_log_step,
              ssm_b_re, ssm_b_im, ssm_c_re, ssm_c_im, ssm_d, ssm_w_glu, na_bias,
              w_br_a, w_br_b, w_br_c, w_out, norm2_g, w_ffn_gu, w_ffn_d, final_g):
    def layer_params(i):
        return dict(w_mod=w_mod[i], b_mod=b_mod[i], norm1_g=norm1_g[i], w_in=w_in[i], qn_g=qn_g[i],
                    kn_g=kn_g[i], ssm_lam_re=ssm_lam_re[i], ssm_lam_im=ssm_lam_im[i],
                    ssm_log_step=ssm_log_step[i], ssm_b_re=ssm_b_re[i], ssm_b_im=ssm_b_im[i],
                    ssm_c_re=ssm_c_re[i], ssm_c_im=ssm_c_im[i], ssm_d=ssm_d[i], ssm_w_glu=ssm_w_glu[i],
                    na_bias=na_bias[i], w_br_a=w_br_a[i], w_br_b=w_br_b[i], w_br_c=w_br_c[i],
                    w_out=w_out[i], norm2_g=norm2_g[i], w_ffn_gu=w_ffn_gu[i], w_ffn_d=w_ffn_d[i])

    xp = x_prompt
    bp = xp.shape[0]
    cvec_ctx = c_ctx[None, :]
    zero_state = jnp.zeros((bp, 2, 2, SSM_GROUPS, SSM_STATE), F32)
    ga_k, ga_v, na_k, na_v, ssm_st = [], [], [], [], []
    for i in range(DEPTH):
        lp = layer_params(i)

        def ctx_mix(qa, ka, va, u, qc, kc, vc, lp=lp):
            oa = blocked_attention(qa, ka, va)
            ob, st = s5_mixer(u, lp, zero_state)
            oc = blocked_attention(qc[:, :, :, None, :], kc, vc)[:, :, :, 0, :]
            return oa, ob, oc, (ka, va, kc, vc, st)

        xp, (ka_i, va_i, kc_i, vc_i, st_i) = trunk_layer(xp, cvec_ctx, lp, ctx_mix)
        ga_k.append(ka_i)
        ga_v.append(va_i)
        na_k.append(kc_i)
        na_v.append(vc_i)
        ssm_st.append(st_i)
    y_prompt = rms_norm(xp, final_g)
    new_ga_k = jnp.stack(ga_k, axis=1)
    new_ga_v = jnp.stack(ga_v, axis=1)
    new_na_k = jnp.stack(na_k, axis=1)
    new_na_v = jnp.stack(na_v, axis=1)
    new_ssm = jnp.stack(ssm_st, axis=1)

    xs = x_sample
    rope = axial_rope_tables(xs.shape[1])
    for i in range(DEPTH):
        lp = layer_params(i)
        cka, cva = cache_ga_k[:, i], cache_ga_v[:, i]
        ckc, cvc = cache_na_k[:, i], cache_na_v[:, i]
        st0 = state_ssm[:, i]

        def lat_mix(qa, ka, va, u, qc, kc, vc, lp=lp, cka=cka, cva=cva, ckc=ckc, cvc=cvc, st0=st0):
            qr = apply_axial_rope(qa, rope)
            kr = apply_axial_rope(ka, rope)
            oa = blocked_attention(qr, jnp.concatenate([cka.astype(kr.dtype), kr], axis=1),
                                   jnp.concatenate([cva.astype(va.dtype), va], axis=1))
            ob, _ = s5_mixer(u, lp, st0)
            oc = neighborhood_attention(qc, kc, vc, ckc.astype(kc.dtype), cvc.astype(vc.dtype), lp["na_bias"])
            return oa, ob, oc, None

        xs, _ = trunk_layer(xs, c, lp, lat_mix)
    y_sample = rms_norm(xs, final_g)

    return (y_prompt, y_sample, new_ga_k, new_ga_v, new_na_k, new_na_v, new_ssm)
```

```python
import math
from contextlib import ExitStack
import numpy as np
import concourse.bass as bass
import concourse.mybir as mybir
from concourse.bass_utils import run_bass_kernel_spmd

F32 = mybir.dt.float32
BF16 = mybir.dt.bfloat16
ALU = mybir.AluOpType
ACTF = mybir.ActivationFunctionType
AX = mybir.AxisListType

D = 1024
KC = 8
TT = 512
NT = 5
LS = 2048
LP = 256
NTOK = 2560
DFF = 2816
NF = 22
DEPTH = 2
EPS = 1e-6
NEG = -30000.0
NCH = 320
GELU_C = 0.7978845608028654
NSLOT = 4
SLOT = 4096

COMPUTE = ("pe", "act", "dve", "pool")


class Op:
    __slots__ = ("eng", "fn", "reads", "writes", "dma", "idx", "deps", "sig", "ticket", "chan", "chan_n", "tag")

    def __init__(self, eng, fn, reads, writes, dma):
        self.eng = eng
        self.fn = fn
        self.reads = reads
        self.writes = writes
        self.dma = dma
        self.deps = []
        self.sig = False
        self.ticket = 0
        self.chan = None
        self.chan_n = 0


class Rec:
    def __init__(self, nc, n_chan=12):
        self.nc = nc
        self.ops = []
        self.last_w = {}
        self.readers = {}
        self.floor = {}
        self.n_chan = n_chan
        self.chan_rr = 0
        self.chan_last = {}
        self.tag = ""
        self.pfx = ""
        self.names = {}

    def op(self, eng, fn, reads=(), writes=(), dma=False, chan=None):
        ps_reads = [k for k in reads if k.startswith("ps")]
        if ps_reads:
            reads = [k for k in reads if not k.startswith("ps")]
            writes = list(writes) + [k for k in ps_reads if k not in writes]
        o = Op(eng, fn, tuple(reads), tuple(writes), dma)
        o.idx = len(self.ops)
        o.tag = self.pfx + self.tag
        deps = set()
        for k in o.reads:
            w = self.last_w.get(k)
            if w is not None:
                deps.add(w)
        for k in o.writes:
            w = self.last_w.get(k)
            if w is not None:
                deps.add(w)
            for r in self.readers.get(k, ()):
                deps.add(r)
        for f in self.floor.values():
            deps.add(f)
        if dma:
            if chan is None:
                chan = self.chan_rr
                self.chan_rr = (self.chan_rr + 1) % self.n_chan
            o.chan = chan
            prev = self.chan_last.get(chan)
            if prev is not None:
                deps.add(prev)
            self.chan_last[chan] = o
        deps.discard(o)
        o.deps = list(deps)
        for k in o.reads:
            self.readers.setdefault(k, []).append(o)
        for k in o.writes:
            self.last_w[k] = o
            self.readers[k] = []
        self.ops.append(o)
        return o

    def barrier(self):
        last = {}
        for o in self.ops:
            key = ("dma", o.chan) if o.dma else o.eng
            last[key] = o
        self.floor = last
        self.last_w = {}
        self.readers = {}

    def emit(self):
        nc = self.nc
        ops = self.ops
        for o in ops:
            for d in o.deps:
                if d.dma:
                    continue
                if d.eng == "pe" and o.eng == "pe" and not o.dma:
                    continue
                d.sig = True
        cnt = {e: 0 for e in COMPUTE}
        chan_cnt = {}
        for o in ops:
            if o.dma:
                chan_cnt[o.chan] = chan_cnt.get(o.chan, 0) + 1
                o.chan_n = chan_cnt[o.chan]
            elif o.sig:
                cnt[o.eng] += 1
                o.ticket = cnt[o.eng]
        with ExitStack() as st:
            sems = {e: st.enter_context(nc.semaphore("s_" + e)) for e in COMPUTE}
            csems = {c: st.enter_context(nc.semaphore("c_%s" % str(c))) for c in sorted(chan_cnt, key=str)}
            block = st.enter_context(nc.Block())
            engs = {"pe": "tensor", "act": "scalar", "dve": "vector", "pool": "gpsimd", "sp": "sync"}

            def run(ename):
                def body(eng):
                    seen = {}
                    for o in ops:
                        if o.eng != ename:
                            continue
                        need = {}
                        for d in o.deps:
                            if d.dma:
                                s = csems[d.chan]
                                v = 16 * d.chan_n
                            else:
                                if d.eng == "pe" and ename == "pe" and not o.dma:
                                    continue
                                s = sems[d.eng]
                                v = d.ticket
                            key = id(s)
                            if v > need.get(key, (None, 0))[1]:
                                need[key] = (s, v)
                        for key, (s, v) in need.items():
                            if seen.get(key, 0) >= v:
                                continue
                            eng.wait_ge(s, v)
                            seen[key] = v
                        ins = o.fn(eng)
                        if self.names is not None:
                            try:
                                self.names[ins.ins.name] = o.tag
                            except Exception:
                                pass
                        if o.dma:
                            ins.then_inc(csems[o.chan], 16)
                        elif o.sig:
                            ins.then_inc(sems[ename], 1)
                    if ename == "sp":
                        for c, n in chan_cnt.items():
                            eng.wait_ge(csems[c], 16 * n)
                        for e in COMPUTE:
                            if cnt[e] > 0:
                                eng.wait_ge(sems[e], cnt[e])
                return body

            for ename, attr in engs.items():
                if ename == "sp" or any(o.eng == ename for o in ops):
                    getattr(block, attr)(run(ename))
        return cnt, chan_cnt


def _fm(x):
    T, F = x.shape
    return np.ascontiguousarray(x.reshape(T, F // 128, 128).transpose(2, 1, 0))


def _wunit(w):
    K, C = w.shape
    return np.ascontiguousarray(w.reshape(K // 128, 128, C).transpose(1, 0, 2))


def _consts():
    c = {}
    c["ident"] = np.eye(128, dtype=np.float32)
    P = np.zeros((64, 64), np.float32)
    for base in (0, 32):
        for i in range(16):
            P[base + 16 + i, base + i] = -1.0
            P[base + i, base + 16 + i] = 1.0
    c["prot"] = P
    t = np.arange(LS)
    row = (t // 64).astype(np.float32)
    col = (t % 64).astype(np.float32)
    inv = (1.0 / (10000.0 ** (np.arange(16, dtype=np.float32) / 16.0))).astype(np.float32)
    ang = np.zeros((64, LS), np.float32)
    for d in range(64):
        pos = row if d < 32 else col
        ang[d] = pos * inv[d % 16]
    c["ropec"] = np.cos(ang).astype(np.float32)
    c["ropes"] = np.sin(ang).astype(np.float32)
    i = np.arange(128)
    mf = np.zeros((128, 8), np.float32)
    mf[i, i % 8] = 1.0
    c["mfwd"] = mf
    sf = np.zeros((128, 16), np.float32)
    sf[i, i // 8] = 1.0
    c["selfwd"] = sf
    jm = np.zeros((128, 8), np.float32)
    jm[i, i // 16] = 1.0
    c["jmask"] = jm
    rs = np.zeros((128, 8, 128), np.float32)
    for g in range(8):
        for p in range(128):
            rs[p, g, g * 16 + (p % 16)] = 1.0
    c["revsel"] = rs
    c["negfill"] = np.full((128, 2560), NEG, np.float32)
    return c


def _prep_core(inp, core):
    b = core // 2
    p0, p1 = 2 * core, 2 * core + 1
    m = {}
    xs = inp["x_sample"][b]
    xp = np.concatenate([inp["x_prompt"][p0], inp["x_prompt"][p1]], axis=0)
    xin = np.stack([_fm(xs[512 * t:512 * t + 512]) for t in range(4)] + [_fm(xp)], axis=0)
    m["xin"] = xin.reshape(NT, 128, KC * TT)
    cv = np.stack([inp["c"][b], inp["c_ctx"]], axis=1)
    m["cvT"] = np.ascontiguousarray(cv.reshape(8, 128, 2).transpose(1, 0, 2)).reshape(128, 16)
    for l in range(DEPTH):
        ck = inp["cache_ga_k"][b, l]
        m["ckaT%d" % l] = np.ascontiguousarray(ck.transpose(2, 1, 0)).reshape(64, 2 * 256)
        cvv = inp["cache_ga_v"][b, l]
        m["cva%d" % l] = np.ascontiguousarray(cvv.reshape(2, 128, 2, 64).transpose(1, 0, 2, 3)).reshape(128, 256)
        ck = inp["cache_na_k"][b, l]
        m["ckcT%d" % l] = np.ascontiguousarray(ck.transpose(2, 1, 0)).reshape(64, 4 * 256)
        cvv = inp["cache_na_v"][b, l]
        m["cvc%d" % l] = np.ascontiguousarray(cvv.reshape(2, 128, 4, 64).transpose(1, 0, 2, 3)).reshape(128, 512)
        st = inp["state_ssm"][b, l]
        h0 = np.zeros((128, 16, 2), np.float32)
        for d in range(2):
            for gp in range(8):
                for gl in range(2):
                    h0[gl * 64:(gl + 1) * 64, d * 8 + gp, :] = st[d, :, 2 * gp + gl, :].T
        m["h0_%d" % l] = h0.reshape(128, 32)
    return m


def _prep_shared(inp):
    m = {}
    c = _consts()
    for k, v in c.items():
        m["c_" + k] = np.ascontiguousarray(v.reshape(v.shape[0], -1))
    for l in range(DEPTH):
        wm = _wunit(inp["w_mod"][l])
        m["wmod%d" % l] = np.ascontiguousarray(
            wm.reshape(128, 8, 12, 512).transpose(2, 0, 1, 3)).reshape(12, 128, 4096)
        win = _wunit(inp["w_in"][l])
        qa, ka, va, u = win[:, :, 0:512], win[:, :, 512:640], win[:, :, 640:768], win[:, :, 768:1024]
        qc, kc, vc = win[:, :, 1024:1280], win[:, :, 1280:1536], win[:, :, 1536:1792]
        g = win[:, :, 1792:]
        m["wu%d" % l] = np.ascontiguousarray(u).reshape(1, 128, 2048)
        m["wkv%d" % l] = np.ascontiguousarray(np.concatenate([ka, va, vc], axis=2)).reshape(1, 128, 4096)
        m["wkc%d" % l] = np.ascontiguousarray(kc).reshape(1, 128, 2048)
        m["wqa%d" % l] = np.ascontiguousarray(qa).reshape(1, 128, 4096)
        m["wqc%d" % l] = np.ascontiguousarray(qc).reshape(1, 128, 2048)
        g4 = g.reshape(128, 8, 3, 8, 128)
        m["wg%d" % l] = np.ascontiguousarray(g4.transpose(3, 0, 1, 2, 4)).reshape(8, 128, 8 * 384)
        wa = inp["w_br_a"][l].reshape(8, 64, 8, 128)
        wb = inp["w_br_b"][l].reshape(2, 128, 8, 128)
        wc = inp["w_br_c"][l].reshape(4, 64, 8, 128)
        br = np.zeros((8, 128, 1792), np.float32)
        for mm in range(8):
            br[mm, 0:64, 0:1024] = wa[:, :, mm, :].transpose(1, 0, 2).reshape(64, 1024)
            br[mm, 0:64, 1024:1536] = wc[:, :, mm, :].transpose(1, 0, 2).reshape(64, 512)
            br[mm, :, 1536:1792] = wb[:, :, mm, :].transpose(1, 0, 2).reshape(128, 256)
        m["wbr%d" % l] = br
        wo = _wunit(inp["w_out"][l])
        m["wout%d" % l] = np.ascontiguousarray(wo.reshape(128, 8, 2, 512).transpose(2, 0, 1, 3)).reshape(2, 128, 4096)
        wgu = _wunit(inp["w_ffn_gu"][l])
        gate = wgu[:, :, 0:DFF].reshape(128, 8, 11, 256)
        up = wgu[:, :, DFF:].reshape(128, 8, 11, 256)
        m["wgu%d" % l] = np.ascontiguousarray(
            np.concatenate([gate, up], axis=3).transpose(2, 0, 1, 3)).reshape(11, 128, 4096)
        wd = _wunit(inp["w_ffn_d"][l])
        m["wd%d" % l] = np.ascontiguousarray(wd.reshape(128, 22, 8, 128).transpose(2, 0, 1, 3)).reshape(8, 128, 22 * 128)
        m["wglu%d" % l] = _wunit(inp["ssm_w_glu"][l]).reshape(1, 128, 512)
        sm = np.zeros((128, 80), np.float32)
        sm[:, 0:48] = inp["b_mod"][l].reshape(48, 128).T
        sm[:, 48:56] = inp["norm1_g"][l].reshape(8, 128).T
        sm[:, 56:64] = inp["norm2_g"][l].reshape(8, 128).T
        sm[:, 64:72] = inp["final_g"].reshape(8, 128).T
        sm[:, 72:74] = inp["ssm_d"][l].reshape(2, 128).T
        sm[0:64, 74] = inp["qn_g"][l]
        sm[0:64, 75] = inp["kn_g"][l]
        m["small%d" % l] = sm
        sp = np.zeros((128, 16, 67), np.float32)
        for d in range(2):
            for gp in range(8):
                for gl in range(2):
                    g_ = 2 * gp + gl
                    rows = slice(gl * 64, gl * 64 + 64)
                    td = d * 8 + gp
                    sp[rows, td, 0] = inp["ssm_lam_re"][l, d, g_]
                    sp[rows, td, 1] = inp["ssm_lam_im"][l, d, g_]
                    sp[rows, td, 2] = inp["ssm_log_step"][l, d, g_]
                    sp[rows, td, 3:19] = inp["ssm_b_re"][l, d, g_]
                    sp[rows, td, 19:35] = inp["ssm_b_im"][l, d, g_]
                    sp[rows, td, 35:51] = inp["ssm_c_re"][l, d, g_].T
                    sp[rows, td, 51:67] = inp["ssm_c_im"][l, d, g_].T
        m["ssmp%d" % l] = sp.reshape(128, 16 * 67)
        m["nab%d" % l] = np.ascontiguousarray(inp["na_bias"][l]).reshape(4, 465)
    return m


IN_SHAPES = None


class Prog:
    def __init__(self, shapes):
        self.nc = nc = bass.Bass("TRN2", target_bir_lowering=False)
        self.R = Rec(nc)
        self.din = {}
        for name, shp in shapes.items():
            self.din[name] = nc.dram_tensor(name, list(shp), F32, kind="ExternalInput").ap()
        self.dout = {}

        def out(name, shp):
            self.dout[name] = nc.dram_tensor(name, list(shp), F32, kind="ExternalOutput").ap()
        out("y_out", [NT, 128, KC * TT])
        out("nk_a", [DEPTH, 64, 2 * 512])
        out("nv_a", [DEPTH, 128, 4 * 128])
        out("nk_c", [DEPTH, 64, 4 * 512])
        out("nv_c", [DEPTH, 128, 4 * 256])
        out("nssm", [DEPTH, 128, 16 * 4])
        self.xs = nc.dram_tensor("xs_scr", [NT, 128, KC * TT], F32).ap()
        self.mq = nc.dram_tensor("mq_scr", [5, 128, 4 * 640], F32).ap()
        self.mt = nc.dram_tensor("mt_scr", [5, 128, 2560], BF16).ap()
        self.uid = 0
        self.wi = 0
        self.alloc()

    def sb(self, name, shape, dt):
        return self.nc.alloc_sbuf_tensor("sb_" + name, list(shape), dt)

    def alloc(self):
        nc = self.nc
        self.ps = [nc.alloc_psum_tensor("ps%d" % i, [128, 512], F32) for i in range(8)]
        self.psi = 0
        self.psum_n = 6
        self.wring = [self.sb("wr%d" % i, [128, SLOT], BF16) for i in range(NSLOT)]
        self.f32r = [self.sb("fr%d" % i, [128, 512], F32) for i in range(8)]
        self.f32i = 0
        self.b16r = [self.sb("br%d" % i, [128, 512], BF16) for i in range(6)]
        self.b16i = 0
        self.xt = self.sb("xt", [128, KC, TT], F32)
        self.hT = self.sb("hT", [128, KC, TT], BF16)
        self.obT = self.sb("obT", [128, 2, NTOK], BF16)
        self.identf = self.sb("identf", [128, 128], F32)
        self.identb = self.sb("identb", [128, 128], BF16)
        self.onesb = self.sb("onesb", [128, 128], BF16)
        self.eaug = self.sb("eaug", [64, 65], BF16)
        self.esel = self.sb("esel", [65, 64], F32)
        self.prot = self.sb("prot", [64, 64], F32)
        self.mfwd = self.sb("mfwd", [128, 8], F32)
        self.selfwd = self.sb("selfwd", [128, 16], BF16)
        self.jmask = self.sb("jmask", [128, 8], F32)
        self.revsel = self.sb("revsel", [128, 8, 128], BF16)
        self.rsc = self.sb("rsc", [65, 512], F32)
        self.osb = self.sb("osb", [64, 512], F32)
        self.pscr = [self.sb("pscr%d" % i, [128, 2, 31], F32) for i in range(2)]
        self.cvT = self.sb("cvT", [128, 8, 2], F32)
        self.silT = self.sb("silT", [128, 8, 2], BF16)
        self.small = self.sb("small", [128, 80], F32)
        self.modv = self.sb("modv", [128, 48, 2], F32)
        self.a1 = self.sb("a1", [128, 8, 2], F32)
        self.a2 = self.sb("a2", [128, 8, 2], F32)
        self.gth = self.sb("gth", [128, 2, 8, 2], F32)
        self.kmax = self.sb("kmax", [65, 8], F32)
        self.ktmp = self.sb("ktmp", [65, 8], F32)
        self.wglu = self.sb("wglu", [128, 2, 256], BF16)
        UNI_BYTES = 109 * 1024
        self.uni = self.sb("uni", [128, UNI_BYTES // 2], BF16)
        self.uni_bytes = UNI_BYTES

    def uview(self, off_bytes, shape, dt, parts=128):
        esz = 4 if dt == F32 else 2
        n = 1
        for s in shape[1:]:
            n *= s
        assert off_bytes % 4 == 0 and off_bytes + n * esz <= self.uni_bytes, (off_bytes, n, esz)
        a = self.uni[0:shape[0], off_bytes // 2: off_bytes // 2 + n * esz // 2]
        if dt == F32:
            a = a.bitcast(F32)
        if len(shape) == 2:
            return a
        if len(shape) == 3:
            return a.rearrange("p (a b) -> p a b", a=shape[1])
        if len(shape) == 4:
            return a.rearrange("p (a b c) -> p a b c", a=shape[1], b=shape[2])
        if len(shape) == 5:
            return a.rearrange("p (a b c d) -> p a b c d", a=shape[1], b=shape[2], c=shape[3])
        raise ValueError

    def f32(self):
        i = self.f32i % len(self.f32r)
        self.f32i += 1
        return self.f32r[i], "fr%d" % i

    def b16(self):
        i = self.b16i % len(self.b16r)
        self.b16i += 1
        return self.b16r[i], "br%d" % i

    def psum(self):
        i = self.psi % self.psum_n
        self.psi += 1
        return self.ps[i], "ps%d" % i

    def mm(self, out, lhsT, rhs, start, stop, reads, writes):
        self.R.op("pe", lambda e: e.matmul(out, lhsT, rhs, start=start, stop=stop), reads, writes)

    def tr(self, out, in_, ident, reads, writes):
        self.R.op("pe", lambda e: e.transpose(out, in_, ident), reads, writes)

    def act(self, out, in_, func, reads, writes, bias=None, scale=None):
        kw = {}
        if bias is not None:
            kw["bias"] = bias
        if scale is not None:
            kw["scale"] = scale
        self.R.op("act", lambda e: e.activation(out=out, in_=in_, func=func, **kw), reads, writes)

    def tt(self, out, a, b, op, reads, writes, eng="dve"):
        self.R.op(eng, lambda e: e.tensor_tensor(out, a, b, op), reads, writes)

    def ts(self, out, a, s1, s2, op0, op1, reads, writes, eng="dve"):
        if s2 is None:
            self.R.op(eng, lambda e: e.tensor_scalar(out, a, s1, None, op0), reads, writes)
        else:
            self.R.op(eng, lambda e: e.tensor_scalar(out, a, s1, s2, op0, op1), reads, writes)

    def stt(self, out, a, s, b, op0, op1, reads, writes):
        self.R.op("dve", lambda e: e.scalar_tensor_tensor(out, a, s, b, op0, op1), reads, writes)

    def recip(self, out, a, reads, writes):
        self.R.op("dve", lambda e: e.reciprocal(out, a), reads, writes)

    def copy(self, out, a, reads, writes, eng="dve"):
        if eng == "act":
            self.R.op("act", lambda e: e.activation(out=out, in_=a, func=ACTF.Copy), reads, writes)
        else:
            self.R.op(eng, lambda e: e.tensor_copy(out, a), reads, writes)

    def memset(self, ap, v, writes, eng="dve"):
        self.R.op(eng, lambda e: e.memset(ap, v), (), writes)

    def dma(self, out, in_, reads, writes, eng="sp", chan=None):
        self.R.op(eng, lambda e: e.dma_start(out=out, in_=in_), reads, writes, dma=True, chan=chan)

    def dbg(self, name, ap2d, key):
        shp = list(ap2d.shape)
        dt = ap2d.dtype
        t = self.nc.dram_tensor("dbg_" + name, shp, dt, kind="ExternalOutput").ap()
        self.dout["dbg_" + name] = t
        self.dma(t, ap2d, [key], ["dbg_" + name])

    def sub(self, i):
        import os
        return int(os.environ.get("MK_SUB", "0")) == i

    def wload(self, src, n):
        s = self.wi % NSLOT
        self.wi += 1
        t = self.wring[s]
        key = "w%d" % s
        self.dma(t[:, 0:n], src, (), [key], eng="pool", chan="w%d" % s)
        return t, key

    def setup(self):
        d = self.din
        ld = self.dma
        fr, fk = self.f32()
        ld(self.identf[:], d["c_ident"], (), ["identf"])
        ld(self.identb[:], d["c_ident"], (), ["identb"], eng="pool", chan="cst")
        self.memset(self.onesb[:], 1.0, ["onesb"])
        self.memset(self.eaug[:], 0.0, ["eaug"])
        self.memset(self.eaug[0:64, 64:65], 1.0, ["eaug"])
        self.memset(self.esel[:], 0.0, ["esel"])
        self.memset(self.esel[64:65, :], 1.0, ["esel"])
        ld(self.prot[:], d["c_prot"], (), ["prot"])
        ld(self.mfwd[:], d["c_mfwd"], (), ["mfwd"])
        ld(self.selfwd[:], d["c_selfwd"], (), ["selfwd"], eng="pool", chan="cst")
        ld(self.jmask[:], d["c_jmask"], (), ["jmask"])
        ld(self.revsel[:].rearrange("p a b -> p (a b)"), d["c_revsel"], (), ["revsel"], eng="pool", chan="cst")
        self.memset(self.rsc[:], 0.0, ["rsc"])
        ld(self.cvT[:].rearrange("p a b -> p (a b)"), d["cvT"], (), ["cvT"])
        cv = self.cvT[:].rearrange("p a b -> p (a b)")
        t1, k1 = self.f32()
        self.act(t1[:, 0:16], cv, ACTF.Exp, ["cvT"], [k1], scale=-1.0)
        self.ts(t1[:, 0:16], t1[:, 0:16], 1.0, None, ALU.add, None, [k1], [k1])
        self.recip(t1[:, 0:16], t1[:, 0:16], [k1], [k1])
        self.tt(self.silT[:].rearrange("p a b -> p (a b)"), t1[:, 0:16], cv, ALU.mult, [k1, "cvT"], ["silT"])

    def layer_params(self, l):
        d = self.din
        self.R.tag = "params"
        self.dma(self.small[:], d["small%d" % l], (), ["small"])
        self.dma(self.wglu[:].rearrange("p a b -> p (a b)"), d["wglu%d" % l][0], (), ["wglu"], eng="pool", chan="cst")
        pm = self.ps[7]
        for u in range(12):
            w, wk = self.wload(d["wmod%d" % l][u], 4096)
            for mm_ in range(4):
                m = 4 * u + mm_
                for k in range(KC):
                    self.mm(pm[:, 2 * m:2 * m + 2], w[:, k * 512 + mm_ * 128: k * 512 + mm_ * 128 + 128],
                            self.silT[:, k, :], k == 0, k == KC - 1, [wk, "silT"], ["ps7"])
        bm = self.small[:, 0:48].unsqueeze(2).to_broadcast([128, 48, 2])
        self.tt(self.modv[:], pm[:, 0:96].rearrange("p (a b) -> p a b", b=2), bm, ALU.add, ["ps7", "small"], ["modv"])
        self.ts(self.gth[:, 0, :, :], self.modv[:, 16:24, :], 0.5, None, ALU.mult, None, ["modv"], ["gth"])
        self.ts(self.gth[:, 1, :, :], self.modv[:, 40:48, :], 0.5, None, ALU.mult, None, ["modv"], ["gth"])
        for (a, an, which, goff) in ((self.a1, "a1", 1, 48), (self.a2, "a2", 4, 56)):
            g = self.small[:, goff:goff + 8].unsqueeze(2).to_broadcast([128, 8, 2])
            self.ts(a[:], self.modv[:, which * 8:which * 8 + 8, :], 1.0, None, ALU.add, None, ["modv"], [an])
            self.tt(a[:], a[:], g, ALU.mult, [an, "small"], [an])

    def modap(self, which, k, cv):
        return self.modv[:, which * 8 + k, cv:cv + 1]

    def rstd_of(self, xt, xkey):
        pst, pk = self.psum()
        for k in range(KC):
            sq, sk = self.b16()
            self.act(sq[:], xt[:, k, :], ACTF.Square, [xkey + str(k)], [sk])
            self.mm(pst[:], self.onesb[:], sq[:], k == 0, k == KC - 1, [sk, "onesb"], [pk])
        r, rk = self.f32()
        self.act(r[:], pst[:], ACTF.Ln, [pk], [rk], bias=self.epsb[:], scale=1.0 / D)
        self.act(r[:], r[:], ACTF.Exp, [rk], [rk], scale=-0.5)
        return r, rk

    def norm_mod(self, xt, xkey, a, which_b, cv, out, okey):
        r, rk = self.rstd_of(xt, xkey)
        an = "a1" if a is self.a1 else "a2"
        for k in range(KC):
            t, tk = self.f32()
            self.stt(t[:], xt[:, k, :], a[:, k, cv:cv + 1], r[:], ALU.mult, ALU.mult, [xkey + str(k), rk, an], [tk])
            self.act(out[:, k, :], t[:], ACTF.Identity, [tk, "modv"], [okey], bias=self.modap(which_b, k, cv))

    def load_x(self, l, t):
        src = self.din["xin"][t] if l == 0 else self.xs[t]
        for k in range(KC):
            self.dma(self.xt[:, k, :], src[:, k * TT:(k + 1) * TT], ["xs%d_%d" % (t, k)], ["xt%d" % k])

    def phase_a1(self, l, V):
        d = self.din
        self.R.tag = "a1"
        for t in range(NT):
            cv = 0 if t < 4 else 1
            self.load_x(l, t)
            self.norm_mod(self.xt, "xt", self.a1, 0, cv, self.hT, "hT")
            w, wk = self.wload(d["wu%d" % l][0], 2048)
            for cc in range(2):
                p, pk = self.psum()
                for k in range(KC):
                    self.mm(p[:], w[:, k * 256 + cc * 128: k * 256 + cc * 128 + 128], self.hT[:, k, :],
                            k == 0, k == KC - 1, [wk, "hT"], [pk])
                self.ts(V["du"][:, cc, t * TT:(t + 1) * TT], p[:], self.small[:, 72 + cc:73 + cc], None, ALU.mult, None,
                        [pk, "small"], ["du"])
            for blk in range(4):
                p, pk = self.psum()
                for k in range(KC):
                    self.mm(p[:, 0:256], self.hT[:, k, blk * 128:(blk + 1) * 128], w[:, k * 256:(k + 1) * 256],
                            k == 0, k == KC - 1, [wk, "hT"], [pk])
                ur = V["urep"][blk % 2]
                uk = "urep%d" % (blk % 2)
                in0 = p[:, 0:256].rearrange("p (g c) -> p g c", g=16).unsqueeze(2).to_broadcast([128, 16, 8, 16])
                in1 = self.mfwd[:].unsqueeze(1).unsqueeze(3).to_broadcast([128, 16, 8, 16])
                self.tt(ur[:], in0, in1, ALU.mult, [pk, "mfwd"], [uk])
                p2, pk2 = self.psum()
                for g in range(16):
                    self.mm(p2[:, g * 16:(g + 1) * 16], ur[:, g, :, :].rearrange("p a b -> p (a b)"), self.selfwd[:],
                            True, True, [uk, "selfwd"], [pk2])
                c0 = 64 * t + 16 * blk
                self.copy(V["U"][:, :, c0:c0 + 16], p2[:, 0:256].rearrange("p (g k) -> p g k", g=16), [pk2], ["U"], eng="act")

    def ssm_pre(self, l, V):
        d = self.din
        self.R.tag = "ssmpre"
        sp = V["ssmp"]
        self.dma(sp[:].rearrange("p a b -> p (a b)"), d["ssmp%d" % l], (), ["ssmp"])
        self.dma(V["h0"][:].rearrange("p a b -> p (a b)"), d["h0_%d" % l], (), ["h0"])
        S = V["S"]
        names = {}

        def s(name):
            if name not in names:
                names[name] = len(names)
                assert len(names) <= 64
            return S[:, names[name], :], "S_" + name

        def tt(o, a, b, op):
            self.tt(o[0], a[0], b[0], op, [a[1], b[1]], [o[1]])

        def ts(o, a, s1, s2, op0, op1=None):
            self.ts(o[0], a[0], s1, s2, op0, op1, [a[1]], [o[1]])

        def cmul(ore, oim, are, aim, bre, bim):
            t1, t2 = s("ct1"), s("ct2")
            tt(t1, are, bre, ALU.mult)
            tt(t2, aim, bim, ALU.mult)
            tt(ore, t1, t2, ALU.subtract)
            t3, t4 = s("ct3"), s("ct4")
            tt(t3, are, bim, ALU.mult)
            tt(t4, aim, bre, ALU.mult)
            tt(oim, t3, t4, ALU.add)

        lre = (sp[:, :, 0], "ssmp")
        lim = (sp[:, :, 1], "ssmp")
        lst = (sp[:, :, 2], "ssmp")
        step = s("step")
        self.act(step[0], lst[0], ACTF.Exp, ["ssmp"], [step[1]])
        zre, zim = s("zre"), s("zim")
        tt(zre, lre, step, ALU.mult)
        tt(zim, lim, step, ALU.mult)
        ts(zre, zre, 1.0 / 64, None, ALU.mult)
        ts(zim, zim, 1.0 / 64, None, ALU.mult)
        are, aim = s("acre0"), s("acim0")
        ts(are, zre, 1.0 / 7, 1.0, ALU.mult, ALU.add)
        ts(aim, zim, 1.0 / 7, None, ALU.mult)
        flip = 0
        for n in (6, 5, 4, 3, 2):
            flip ^= 1
            nre, nim = s("acre%d" % flip), s("acim%d" % flip)
            cmul(nre, nim, zre, zim, are, aim)
            ts(nre, nre, 1.0 / n, 1.0, ALU.mult, ALU.add)
            ts(nim, nim, 1.0 / n, None, ALU.mult)
            are, aim = nre, nim
        wre, wim = s("wre0"), s("wim0")
        cmul(wre, wim, zre, zim, are, aim)
        flip = 0
        for _ in range(6):
            flip ^= 1
            nre, nim = s("wre%d" % flip), s("wim%d" % flip)
            t1, t2, t3 = s("dt1"), s("dt2"), s("dt3")
            tt(t1, wre, wre, ALU.mult)
            tt(t2, wim, wim, ALU.mult)
            tt(t1, t1, t2, ALU.subtract)
            self.stt(nre[0], wre[0], 2.0, t1[0], ALU.mult, ALU.add, [wre[1], t1[1]], [nre[1]])
            ts(t3, wre, 1.0, None, ALU.add)
            tt(t3, t3, wim, ALU.mult)
            ts(nim, t3, 2.0, None, ALU.mult)
            wre, wim = nre, nim
        lbre, lbim = s("lbre"), s("lbim")
        ts(lbre, wre, 1.0, None, ALU.add)
        ts(lbim, wim, 1.0, None, ALU.mult)
        den, t1 = s("den"), s("kt1")
        tt(den, lre, lre, ALU.mult)
        tt(t1, lim, lim, ALU.mult)
        tt(den, den, t1, ALU.add)
        self.recip(den[0], den[0], [den[1]], [den[1]])
        ire, iim = s("ire"), s("iim")
        tt(ire, lre, den, ALU.mult)
        tt(iim, lim, den, ALU.mult)
        ts(iim, iim, -1.0, None, ALU.mult)
        kre, kim = s("kre"), s("kim")
        cmul(kre, kim, wre, wim, ire, iim)
        Bre, Bim = sp[:, :, 3:19], sp[:, :, 19:35]
        bb = V["bbar"]
        T1, T2 = V["tmpA"], V["tmpB"]
        kreb = kre[0].unsqueeze(2).to_broadcast([128, 16, 16])
        kimb = kim[0].unsqueeze(2).to_broadcast([128, 16, 16])
        ta, tb = T1[:].rearrange("p a b c -> p (a b c)")[:, 0:256].rearrange("p (a b) -> p a b", a=16), \
            T2[:].rearrange("p a b c -> p (a b c)")[:, 0:256].rearrange("p (a b) -> p a b", a=16)
        self.tt(ta, Bre, kreb, ALU.mult, ["ssmp", kre[1]], ["tmpA"])
        self.tt(tb, Bim, kimb, ALU.mult, ["ssmp", kim[1]], ["tmpB"])
        self.tt(bb[:, 0, :, :], ta, tb, ALU.subtract, ["tmpA", "tmpB"], ["bbar"])
        self.tt(ta, Bim, kreb, ALU.mult, ["ssmp", kre[1]], ["tmpA"])
        self.tt(tb, Bre, kimb, ALU.mult, ["ssmp", kim[1]], ["tmpB"])
        self.tt(bb[:, 1, :, :], ta, tb, ALU.add, ["tmpA", "tmpB"], ["bbar"])
        L = V["L"]
        pre, pim = lbre, lbim
        for n in range(1, 9):
            if n > 1:
                nre, nim = s("pre%d" % (n % 2)), s("pim%d" % (n % 2))
                cmul(nre, nim, pre, pim, lbre, lbim)
                pre, pim = nre, nim
            for r, src in ((0, pre), (1, pim)):
                self.copy(L[:, r, 0:8, n - 1], src[0][:, 0:8], [src[1]], ["L"])
                self.copy(L[:, r, 8:16, 8 - n], src[0][:, 8:16], [src[1]], ["L"])
        lamp = V["lamp"]
        qre, qim = pre, pim
        for sidx in range(8):
            if sidx > 0:
                nre, nim = s("qre%d" % (sidx % 2)), s("qim%d" % (sidx % 2))
                cmul(nre, nim, qre, qim, qre, qim)
                qre, qim = nre, nim
            self.copy(lamp[:, :, sidx, 0], qre[0], [qre[1]], ["lamp"])
            self.copy(lamp[:, :, sidx, 1], qim[0], [qim[1]], ["lamp"])
            self.ts(lamp[:, :, sidx, 2], qim[0], -1.0, None, ALU.mult, None, [qim[1]], ["lamp"])
        h0 = V["h0"]
        lh0 = V["lh0"]
        lamre, lamim = (lamp[:, :, 0, 0], "lamp"), (lamp[:, :, 0, 1], "lamp")
        cmul((lh0[:, :, 0], "lh0"), (lh0[:, :, 1], "lh0"), lamre, lamim, (h0[:, :, 0], "h0"), (h0[:, :, 1], "h0"))
        Cre, Cim = sp[:, :, 35:51], sp[:, :, 51:67]
        WH = V["WH"]
        for hf in range(2):
            tsl = slice(hf * 8, hf * 8 + 8)
            creb = Cre[:, tsl, :].unsqueeze(2).to_broadcast([128, 8, 8, 16])
            cimb = Cim[:, tsl, :].unsqueeze(2).to_broadcast([128, 8, 8, 16])
            Lreb = L[:, 0, tsl, :].unsqueeze(3).to_broadcast([128, 8, 8, 16])
            Limb = L[:, 1, tsl, :].unsqueeze(3).to_broadcast([128, 8, 8, 16])
            self.tt(T1[:], creb, Lreb, ALU.mult, ["ssmp", "L"], ["tmpA"])
            self.tt(T2[:], cimb, Limb, ALU.mult, ["ssmp", "L"], ["tmpB"])
            self.tt(WH[:, tsl, 0, :].rearrange("p a (j c) -> p a j c", j=8), T1[:], T2[:], ALU.subtract, ["tmpA", "tmpB"], ["WH"])
            self.tt(T1[:], creb, Limb, ALU.mult, ["ssmp", "L"], ["tmpA"])
            self.tt(T2[:], cimb, Lreb, ALU.mult, ["ssmp", "L"], ["tmpB"])
            self.tt(T1[:], T1[:], T2[:], ALU.add, ["tmpA", "tmpB"], ["tmpA"])
            self.ts(WH[:, tsl, 1, :].rearrange("p a (j c) -> p a j c", j=8), T1[:], -1.0, None, ALU.mult, None, ["tmpA"], ["WH"])
        cimn = V["cimn"]
        self.ts(cimn[:], Cim, -1.0, None, ALU.mult, None, ["ssmp"], ["cimn"])
        H = V["H"]
        KM = V["KM"]
        WS = V["WS"]
        for grp in range(4):
            tds = [grp * 4 + i for i in range(4)]
            pk0, pk1 = self.ps[6], self.ps[7]
            for step_i in range(8):
                cur, prv = step_i % 2, (step_i + 1) % 2
                for sl, td in enumerate(tds):
                    dirn = td // 8
                    pos = step_i if dirn == 0 else 7 - step_i
                    hk_c, hk_p = "H%d_%d" % (cur, sl), "H%d_%d" % (prv, sl)
                    Hc_re, Hc_im = H[:, cur, sl, 0, :], H[:, cur, sl, 1, :]
                    Hp_re, Hp_im = H[:, prv, sl, 0, :], H[:, prv, sl, 1, :]
                    lr, li = lbre[0][:, td:td + 1], lbim[0][:, td:td + 1]
                    if step_i == 0:
                        self.memset(H[:, cur, sl, :, :], 0.0, [hk_c])
                    else:
                        t, tk = self.f32()
                        self.ts(t[:, 0:128], Hp_im, li, None, ALU.mult, None, [hk_p, lbim[1]], [tk])
                        self.stt(Hc_re, Hp_re, lr, t[:, 0:128], ALU.mult, ALU.subtract, [hk_p, tk, lbre[1]], [hk_c])
                        t2, tk2 = self.f32()
                        self.ts(t2[:, 0:128], Hp_re, li, None, ALU.mult, None, [hk_p, lbim[1]], [tk2])
                        self.stt(Hc_im, Hp_im, lr, t2[:, 0:128], ALU.mult, ALU.add, [hk_p, tk2, lbre[1]], [hk_c])
                    self.copy(H[:, cur, sl, :, pos * 16:(pos + 1) * 16], bb[:, :, td, :], ["bbar", hk_c], [hk_c])
                    for gl in range(2):
                        reg = sl * 2 + gl
                        pt = pk0 if reg < 4 else pk1
                        pkey = "ps6" if reg < 4 else "ps7"
                        o = pt[:, (reg % 4) * 128 + pos * 16:(reg % 4) * 128 + pos * 16 + 16]
                        rows = slice(gl * 64, gl * 64 + 64)
                        self.mm(o, H[rows, cur, sl, 0, :], sp[rows, td, 35:51], True, False, [hk_c, "ssmp"], [pkey])
                        self.mm(o, H[rows, cur, sl, 1, :], cimn[rows, td, :], False, True, [hk_c, "cimn"], [pkey])
            for sl, td in enumerate(tds):
                dirn, gp = td // 8, td % 8
                for gl in range(2):
                    reg = sl * 2 + gl
                    pt = pk0 if reg < 4 else pk1
                    pkey = "ps6" if reg < 4 else "ps7"
                    self.copy(KM[:, 2 * gp + gl, dirn, :], pt[:, (reg % 4) * 128:(reg % 4) * 128 + 128], [pkey], ["KM"], eng="act")
                for r in range(2):
                    p, pk = self.psum()
                    self.tr(p[:, 0:128], H[:, 1, sl, r, :], self.identf[:], ["H1_%d" % sl, "identf"], [pk])
                    self.copy(WS[:, td, r, :], p[:, 0:128], [pk], ["WS"], eng="act")

    def ssm_main(self, l, V):
        U, WS, WH, KM = V["U"], V["WS"], V["WH"], V["KM"]
        self.R.tag = "scan"
        lamp, lh0, h0 = V["lamp"], V["lh0"], V["h0"]
        Ep = V["Eprev"]
        nss = V["nss"]
        for dirn, half in ((0, 0), (0, 1), (1, 0), (1, 1)):
            E = [V["E0"], V["E1"]]
            cur = 0
            gps = [half * 4 + i for i in range(4)]
            for gi, gp in enumerate(gps):
                td = dirn * 8 + gp
                pr, prk = self.psum()
                pi, pik = self.psum()
                for gl in range(2):
                    rows = slice(gl * 64, gl * 64 + 64)
                    g = 2 * gp + gl
                    self.mm(pr[rows, 0:NCH], WS[:, td, 0, gl * 64:gl * 64 + 64], U[:, g, :], True, True, ["WS", "U"], [prk])
                    self.mm(pi[rows, 0:NCH], WS[:, td, 1, gl * 64:gl * 64 + 64], U[:, g, :], True, True, ["WS", "U"], [pik])
                self.copy(E[0][:, gi, 0, :], pr[:, 0:NCH], [prk], ["E0s", "E0p"], eng="act")
                self.copy(E[0][:, gi, 1, :], pi[:, 0:NCH], [pik], ["E0s", "E0p"], eng="act")
            col = 0 if dirn == 0 else 255
            td0 = dirn * 8 + half * 4
            self.tt(E[0][:, :, :, col], E[0][:, :, :, col], lh0[:, td0:td0 + 4, :], ALU.add, ["E0s", "lh0"], ["E0s"])
            for sidx in range(8):
                sh = 1 << sidx
                src, dst = E[cur], E[1 - cur]
                sk, dk = "E%ds" % cur, "E%ds" % (1 - cur)
                skp, dkp = "E%dp" % cur, "E%dp" % (1 - cur)
                ranges = [(0, 256, 1)]
                if sh < 32:
                    ranges.append((256, 32, 2))
                for gi, gp in enumerate(gps):
                    td = dirn * 8 + gp
                    lr = lamp[:, td, sidx, 0:1]
                    li = lamp[:, td, sidx, 1:2]
                    lin = lamp[:, td, sidx, 2:3]
                    for (base, ln, nseq) in ranges:
                        def v(buf, r, off, n):
                            a = buf[:, gi, r, base:base + ln * nseq]
                            if nseq == 2:
                                return a.rearrange("p (s k) -> p s k", s=2)[:, :, off:off + n]
                            return a[:, off:off + n]
                        n = ln - sh
                        if dirn == 0:
                            o_off, s_off, keep_off = sh, 0, 0
                        else:
                            o_off, s_off, keep_off = 0, sh, ln - sh
                        if nseq == 1:
                            t, tk = self.f32()
                            tv3 = t[:, 0:2 * n].rearrange("p (r k) -> p r k", r=2)
                            self.stt(tv3, src[:, gi, :, base + s_off:base + s_off + n], lr,
                                     src[:, gi, :, base + o_off:base + o_off + n], ALU.mult, ALU.add, [sk, "lamp"], [tk])
                            self.stt(v(dst, 0, o_off, n), v(src, 1, s_off, n), lin, tv3[:, 0, :], ALU.mult, ALU.add, [sk, tk, "lamp"], [dk])
                            self.stt(v(dst, 1, o_off, n), v(src, 0, s_off, n), li, tv3[:, 1, :], ALU.mult, ALU.add, [sk, tk, "lamp"], [dk])
                            self.copy(dst[:, gi, :, base + keep_off:base + keep_off + sh], src[:, gi, :, base + keep_off:base + keep_off + sh],
                                      [sk], [dk])
                            continue
                        pa, pb_ = self.pscr[0][:, :, 0:n], self.pscr[1][:, :, 0:n]

                        def pts(o_, i_, sc, rd, wr):
                            self.R.op("pool", lambda e: e.tensor_scalar(o_, i_, sc, 0.0, ALU.mult, ALU.add), rd, wr)

                        def ptt(o_, i0, i1, rd, wr):
                            self.R.op("pool", lambda e: e.tensor_tensor(o_, i0, i1, ALU.add), rd, wr)
                        pts(pa, v(src, 0, s_off, n), lr, [skp, "lamp"], ["pscr0"])
                        ptt(pa, pa, v(src, 0, o_off, n), [skp, "pscr0"], ["pscr0"])
                        pts(pb_, v(src, 1, s_off, n), lin, [skp, "lamp"], ["pscr1"])
                        ptt(v(dst, 0, o_off, n), pa, pb_, ["pscr0", "pscr1"], [dkp])
                        pts(pa, v(src, 1, s_off, n), lr, [skp, "lamp"], ["pscr0"])
                        ptt(pa, pa, v(src, 1, o_off, n), [skp, "pscr0"], ["pscr0"])
                        pts(pb_, v(src, 0, s_off, n), li, [skp, "lamp"], ["pscr1"])
                        ptt(v(dst, 1, o_off, n), pa, pb_, ["pscr0", "pscr1"], [dkp])
                        for r in range(2):
                            self.copy(v(dst, r, keep_off, sh), v(src, r, keep_off, sh), [skp], [dkp], eng="pool")
                cur = 1 - cur
            Es = E[cur]
            esk = "E%ds" % cur
            Epr = E[1]
            epk = "E1p"
            for gi, gp in enumerate(gps):
                td = dirn * 8 + gp
                for r in range(2):
                    pv = Ep[:, td, r, 256:320].rearrange("p (s k) -> p s k", s=2)
                    sv = Epr[:, gi, r, 256:320].rearrange("p (s k) -> p s k", s=2)
                    if dirn == 0:
                        self.copy(Ep[:, td, r, 1:256], Es[:, gi, r, 0:255], [esk], ["Eprev"], eng="act")
                        self.copy(Ep[:, td, r, 0:1], h0[:, td, r:r + 1], ["h0"], ["Eprev"])
                        self.copy(pv[:, :, 1:32], sv[:, :, 0:31], [epk], ["Eprev"], eng="act")
                        self.memset(pv[:, :, 0:1], 0.0, ["Eprev"])
                        self.copy(nss[:, td, :, r], sv[:, :, 31], [epk], ["nss"])
                    else:
                        self.copy(Ep[:, td, r, 0:255], Es[:, gi, r, 1:256], [esk], ["Eprev"], eng="act")
                        self.copy(Ep[:, td, r, 255:256], h0[:, td, r:r + 1], ["h0"], ["Eprev"])
                        self.copy(pv[:, :, 0:31], sv[:, :, 1:32], [epk], ["Eprev"], eng="act")
                        self.memset(pv[:, :, 31:32], 0.0, ["Eprev"])
                        self.copy(nss[:, td, :, r], sv[:, :, 0], [epk], ["nss"])
        self.R.barrier()
        self.R.tag = "ssmout"
        self.dma(self.dout["nssm"][l], nss[:].rearrange("p a b c -> p (a b c)"), ["nss"], ["o_nssm%d" % l])
        Ysb = V["Ysb"]
        for g in range(16):
            gp, gl = g // 2, g % 2
            rows = slice(gl * 64, gl * 64 + 64)
            p, pk = self.psum()
            first = True
            for dirn in range(2):
                td = dirn * 8 + gp
                self.mm(p[:, 0:NCH], KM[:, g, dirn, :], U[:, g, :], first, False, ["KM", "U"], [pk])
                first = False
                self.mm(p[:, 0:NCH], WH[rows, td, 0, :], Ep[rows, td, 0, :], False, False, ["WH", "Eprev"], [pk])
                self.mm(p[:, 0:NCH], WH[rows, td, 1, :], Ep[rows, td, 1, :], False, dirn == 1, ["WH", "Eprev"], [pk])
            self.copy(Ysb[:, g, :], p[:, 0:NCH], [pk], ["Ysb"], eng="act")
        for t in range(NT):
            c0 = 64 * t
            yg = []
            for hh in range(2):
                p, pk = self.psum()
                for g8 in range(8):
                    g = hh * 8 + g8
                    rep, rk = self.b16()
                    in0 = Ysb[:, g, c0:c0 + 64].unsqueeze(2).to_broadcast([128, 64, 8])
                    in1 = self.jmask[:].unsqueeze(1).to_broadcast([128, 64, 8])
                    self.tt(rep[:].rearrange("p (k j) -> p k j", j=8), in0, in1, ALU.mult, ["Ysb", "jmask"], [rk])
                    self.mm(p[:], self.revsel[:, g8, :], rep[:], g8 == 0, g8 == 7, ["revsel", rk], [pk])
                yv, yk = self.f32()
                self.tt(yv[:], p[:], V["du"][:, hh, t * TT:(t + 1) * TT], ALU.add, [pk, "du"], [yk])
                y2, y2k = self.f32()
                self.act(y2[:], yv[:], ACTF.Square, [yk], [y2k])
                self.ts(y2[:], y2[:], 0.044715, 1.0, ALU.mult, ALU.add, [y2k], [y2k])
                self.tt(y2[:], y2[:], yv[:], ALU.mult, [y2k, yk], [y2k])
                self.act(y2[:], y2[:], ACTF.Tanh, [y2k], [y2k], scale=GELU_C)
                ygf, ygb = V["ygf"], V["ygb"]
                self.stt(ygf[:, hh, :], y2[:], 1.0, yv[:], ALU.add, ALU.mult, [y2k, yk], ["ygf%d" % hh])
                self.copy(ygb[:, hh, :], ygf[:, hh, :], ["ygf%d" % hh], ["ygb%d" % hh], eng="act")
                yg.append((ygf[:, hh, :], "ygf%d" % hh, ygb[:, hh, :], "ygb%d" % hh))
            for oc in range(2):
                p, pk = self.psum()
                for kc in range(2):
                    self.mm(p[:], self.wglu[:, kc, oc * 128:(oc + 1) * 128], yg[kc][2], kc == 0, kc == 1,
                            ["wglu", yg[kc][3]], [pk])
                e, ek = self.f32()
                self.act(e[:], p[:], ACTF.Tanh, [pk], [ek], scale=0.25)
                self.stt(e[:], e[:], 1.0, yg[oc][0], ALU.add, ALU.mult, [ek, yg[oc][1]], [ek])
                self.ts(self.obT[:, oc, t * TT:(t + 1) * TT], e[:], 0.25, None, ALU.mult, None, [ek], ["obT"])

    def build_masks(self, l, V):
        nab = self.din["nab%d" % l]
        self.R.tag = "masks"
        for pi in range(5):
            self.dma(self.mq[pi], self.din["c_negfill"], (), ["mqf%d" % pi])
        W0 = {0: (0, 0), 1: (0, 0), 2: (0, 1), 3: (2, 2), 4: (2, 2)}
        DR0 = {0: (7, 6), 1: (5, 4), 2: (3, 3), 3: (3, 2), 4: (1, 0)}
        wkeys = {pi: [] for pi in range(5)}
        for pi in range(5):
            for i in range(2):
                w0, dr0 = W0[pi][i], DR0[pi][i]
                regs = [(0, 8, 0, 15, -1, 0), (8, 49, 0, 7, 0, 1), (57, 7, 48, 6, -1, 0)]
                for ri, (q0, nq, cs0, dc0, sq, dq) in enumerate(regs):
                    for h in range(4):
                        src = bass.AP(nab.tensor, h * 465 + dr0 * 31 + dc0, [[31, 8], [sq, nq], [1, 16]])
                        dst = bass.AP(self.mq.tensor, pi * 128 * 2560 + (i * 64 + q0) * 2560 + h * 640 + w0 * 64 + cs0,
                                      [[64, 8], [2560 + dq, nq], [1, 16]])
                        wk = "mqw%d_%d_%d_%d" % (pi, i, ri, h)
                        wkeys[pi].append(wk)
                        self.dma(dst, src, ["mqf%d" % pi], [wk])
        for pi in range(5):
            mqs = V["mqs"][pi % 2]
            mts = V["mts"][pi % 2]
            mqk, mtk = "mqs%d" % (pi % 2), "mts%d" % (pi % 2)
            self.dma(mqs[:].rearrange("p a b -> p (a b)"), self.mq[pi], wkeys[pi] + ["mqf%d" % pi], [mqk])
            for h in range(4):
                for half in range(2):
                    p, pk = self.psum()
                    njj = 4 if half == 0 else 1
                    for jj in range(njj):
                        j = half * 4 + jj
                        self.tr(p[:, jj * 128:(jj + 1) * 128], mqs[:, h, j * 128:(j + 1) * 128], self.identf[:],
                                [mqk, "identf"], [pk])
                    self.copy(mts[:, h, half * 4:half * 4 + njj, :].rearrange("p a b -> p (a b)"), p[:, 0:njj * 128], [pk], [mtk],
                              eng="act")
            self.dma(self.mt[pi], mts[:].rearrange("p a b c -> p (a b c)"), [mtk], ["mt%d" % pi])

    def kv_bound_update(self, src_ap, skey, col, n, first):
        sq, sk = self.b16()
        self.act(sq[0:64, 0:n], src_ap, ACTF.Square, [skey], [sk])
        p, pk = self.psum()
        self.mm(p[0:65, 0:n], self.eaug[:], sq[0:64, 0:n], True, True, [sk, "eaug"], [pk])
        if first:
            self.R.op("dve", lambda e: e.reduce_max(self.kmax[64:65, col:col + 1], p[64:65, 0:n], AX.X), [pk], ["kmax"])
        else:
            self.R.op("dve", lambda e: e.reduce_max(self.ktmp[64:65, col:col + 1], p[64:65, 0:n], AX.X), [pk], ["ktmp"])
            self.tt(self.kmax[64:65, col:col + 1], self.kmax[64:65, col:col + 1], self.ktmp[64:65, col:col + 1], ALU.max,
                    ["kmax", "ktmp"], ["kmax"])

    def head_norm(self, p, pk, gcol):
        sq, sk = self.b16()
        self.act(sq[0:64, :], p[0:64, :], ACTF.Square, [pk], [sk])
        p2, pk2 = self.psum()
        self.mm(p2[0:64, :], self.onesb[0:64, 0:64], sq[0:64, :], True, True, [sk, "onesb"], [pk2])
        r, rk = self.f32()
        self.act(r[0:64, :], p2[0:64, :], ACTF.Ln, [pk2], [rk], bias=self.epsb[0:64, :], scale=1.0 / 64)
        self.act(r[0:64, :], r[0:64, :], ACTF.Exp, [rk], [rk], scale=-0.5)
        o, ok = self.f32()
        self.stt(o[0:64, :], p[0:64, :], self.small[0:64, gcol:gcol + 1], r[0:64, :], ALU.mult, ALU.mult, [pk, rk, "small"], [ok])
        return o, ok

    def rope(self, x, xk, V, t):
        p, pk = self.psum()
        self.mm(p[0:64, :], self.prot[:], x[0:64, :], True, True, [xk, "prot"], [pk])
        t1, k1 = self.f32()
        self.tt(t1[0:64, :], x[0:64, :], V["rope"][:, 0, :], ALU.mult, [xk, "rope"], [k1])
        t2, k2 = self.f32()
        self.tt(t2[0:64, :], p[0:64, :], V["rope"][:, 1, :], ALU.mult, [pk, "rope"], [k2])
        self.tt(t1[0:64, :], t1[0:64, :], t2[0:64, :], ALU.add, [k1, k2], [k1])
        return t1, k1

    def load_rope(self, V, t):
        self.dma(V["rope"][:, 0, :], self.din["c_ropec"][:, t * TT:(t + 1) * TT], (), ["rope"])
        self.dma(V["rope"][:, 1, :], self.din["c_ropes"][:, t * TT:(t + 1) * TT], (), ["rope"])

    def init_stores(self, l, V, sample):
        d = self.din
        KaT, Va, KcT, Vc = V["KaT"], V["Va"], V["KcT"], V["Vc"]
        self.memset(KaT[64:128, :, :], 0.0, ["KaT"])
        self.memset(KcT[64:128, :, :], 0.0, ["KcT"])
        self.memset(KaT[64:65, :, :], 1.0, ["KaT"])
        self.memset(KcT[64:65, :, :], 1.0, ["KcT"])
        self.memset(V["qaug"][0][64:128, :], 0.0, ["qaug0"])
        self.memset(V["qaug"][1][64:128, :], 0.0, ["qaug1"])
        self.memset(Va[:].rearrange("p a b c -> p (a b c)"), 0.0, ["Va"])
        self.memset(Vc[:].rearrange("p a b c -> p (a b c)"), 0.0, ["Vc"], eng="pool")
        self.memset(Va[:, :, :, 64:65], 1.0, ["Va"])
        self.memset(Vc[:, :, :, 64:65], 1.0, ["Vc"])
        if sample:
            for g in range(2):
                self.dma(KaT[0:64, g, 0:256], d["ckaT%d" % l][:, g * 256:(g + 1) * 256], (), ["KaT"], eng="pool", chan="cst")
            for h in range(4):
                self.dma(KcT[0:64, h, 0:256], d["ckcT%d" % l][:, h * 256:(h + 1) * 256], (), ["KcT"], eng="pool", chan="cst")
            for a_ in range(2):
                self.dma(Va[:, a_, 0:2, 0:64], d["cva%d" % l][:, a_ * 128:(a_ + 1) * 128].rearrange("p (g e) -> p g e", g=2), (), ["Va"], eng="pool", chan="cst")
                self.dma(Vc[:, a_, 0:4, 0:64], d["cvc%d" % l][:, a_ * 256:(a_ + 1) * 256].rearrange("p (g e) -> p g e", g=4), (), ["Vc"], eng="pool", chan="cst")
            for g in range(2):
                self.kv_bound_update(KaT[0:64, g, 0:256], "KaT", g, 256, True)
            for h in range(4):
                self.kv_bound_update(KcT[0:64, h, 0:256], "KcT", 2 + h, 256, True)

    def phase_a2(self, l, t, V):
        d = self.din
        self.R.tag = "a2"
        sample = t < 4
        cv = 0 if sample else 1
        KaT, Va, KcT, Vc = V["KaT"], V["Va"], V["KcT"], V["Vc"]
        self.load_x(l, t)
        self.norm_mod(self.xt, "xt", self.a1, 0, cv, self.hT, "hT")
        w, wk = self.wload(d["wkv%d" % l][0], 4096)
        w2, wk2 = self.wload(d["wkc%d" % l][0], 2048)
        if self.sub(1): return
        if sample:
            self.load_rope(V, t)
            key0 = 256 + t * TT
        else:
            key0 = 0
        if self.sub(2): return
        for g in range(2):
            p, pk = self.psum()
            for k in range(KC):
                self.mm(p[0:64, :], w[:, k * 512 + g * 64:k * 512 + g * 64 + 64], self.hT[:, k, :], k == 0, k == KC - 1,
                        [wk, "hT"], [pk])
            if self.sub(3): return
            kn, knk = self.head_norm(p, pk, 75)
            if self.sub(4): return
            if sample:
                kn, knk = self.rope(kn, knk, V, t)
            else:
                self.dma(self.dout["nk_a"][l][:, g * 512:(g + 1) * 512], kn[0:64, :], [knk], ["o_nka%d_%d" % (l, g)])
            if self.sub(5): return
            self.copy(KaT[0:64, g, key0:key0 + TT], kn[0:64, :], [knk], ["KaT"], eng="act")
            if self.sub(6): return
            first = (not sample)
            self.kv_bound_update(kn[0:64, :], knk, g, TT, first)
        if self.sub(7): return
        for h in range(4):
            p, pk = self.psum()
            for k in range(KC):
                self.mm(p[0:64, :], w2[:, k * 256 + h * 64:k * 256 + h * 64 + 64], self.hT[:, k, :], k == 0, k == KC - 1,
                        [wk2, "hT"], [pk])
            kf, kfk = self.f32()
            self.copy(kf[0:64, :], p[0:64, :], [pk], [kfk], eng="act")
            if not sample:
                self.dma(self.dout["nk_c"][l][:, h * 512:(h + 1) * 512], kf[0:64, :], [kfk], ["o_nkc%d_%d" % (l, h)])
            self.copy(KcT[0:64, h, key0:key0 + TT], kf[0:64, :], [kfk], ["KcT"])
            self.kv_bound_update(kf[0:64, :], kfk, 2 + h, TT, not sample)
        if self.sub(8): return
        for blk in range(4):
            p, pk = self.psum()
            for k in range(KC):
                self.mm(p[:, 0:384], self.hT[:, k, blk * 128:(blk + 1) * 128], w[:, k * 512 + 128:k * 512 + 512],
                        k == 0, k == KC - 1, [wk, "hT"], [pk])
            ch = (2 + 4 * t + blk) if sample else blk
            if self.sub(9): return
            self.copy(Va[:, ch, 0:2, 0:64], p[:, 0:128].rearrange("p (g e) -> p g e", g=2), [pk], ["Va"], eng="act")
            if self.sub(10): return
            self.copy(Vc[:, ch, 0:4, 0:64], p[:, 128:384].rearrange("p (g e) -> p g e", g=4), [pk], ["Vc"])
            if not sample:
                vf, vfk = self.f32()
                self.copy(vf[:, 0:384], p[:, 0:384], [pk], [vfk])
                self.dma(self.dout["nv_a"][l][:, blk * 128:(blk + 1) * 128], vf[:, 0:128], [vfk], ["o_nva%d_%d" % (l, blk)])
                self.dma(self.dout["nv_c"][l][:, blk * 256:(blk + 1) * 256], vf[:, 128:384], [vfk], ["o_nvc%d_%d" % (l, blk)])

    def q_bound(self, qsrc, qk, qa, qak, col, n):
        sq, sk = self.b16()
        self.act(sq[0:64, 0:n], qsrc, ACTF.Square, [qk], [sk])
        p, pk = self.psum()
        self.mm(p[0:65, 0:n], self.eaug[:], sq[0:64, 0:n], True, True, [sk, "eaug"], [pk])
        r, rk = self.f32()
        self.act(r[64:65, 0:n], p[64:65, 0:n], ACTF.Ln, [pk, "kmax"], [rk], scale=self.kmax[64:65, col:col + 1],
                 bias=self.tinyb[64:65, :])
        self.act(r[64:65, 0:n], r[64:65, 0:n], ACTF.Exp, [rk], [rk], scale=0.5)
        self.ts(qa[64:65, 0:n], r[64:65, 0:n], -0.125, None, ALU.mult, None, [rk], [qak])

    def attn_core(self, qa, qak, n, kch, vch, dst, dkey, po, pok, masks=None, ocol=0):
        nchk = len(kch)
        for j in range(nchk):
            ps_, psk = self.psum()
            has_mask = masks is not None and masks[j] is not None
            self.mm(ps_[:, 0:n], kch[j][0], qa[0:65, 0:n], True, not has_mask, [kch[j][1], qak], [psk])
            if has_mask:
                self.mm(ps_[:, 0:n], self.identb[:], masks[j][0], False, True, ["identb", masks[j][1]], [psk])
            pt, ptk = self.b16()
            self.act(pt[:, 0:n], ps_[:, 0:n], ACTF.Exp, [psk], [ptk])
            self.mm(po[0:65, ocol:ocol + n], vch[j][0], pt[:, 0:n], j == 0, j == nchk - 1, [vch[j][1], ptk], [pok])

    def attn_stream(self, jobs, look=2, bg=None, tag=""):
        groups = []
        for job in jobs:
            n = job["n"]
            per = 512 // n
            nch = len(job["kch"])
            for c0 in range(0, nch, per):
                groups.append((job, list(range(c0, min(c0 + per, nch)))))
        pend = []

        def emit_pv(item):
            job, chunks, pt, ptk = item
            n = job["n"]
            nch = len(job["kch"])
            for i, c in enumerate(chunks):
                self.mm(job["po"][0:128, job["ocol"]:job["ocol"] + n], job["vch"][c][0], pt[:, i * n:(i + 1) * n],
                        c == 0, c == nch - 1, [job["vch"][c][1], ptk], [job["pok"]])
        for gi_, (job, chunks) in enumerate(groups):
            if bg and ((gi_ >= 2 and gi_ % 2 == 0) or len(groups) < 12):
                for g_ in list(bg):
                    try:
                        next(g_)
                    except StopIteration:
                        bg.remove(g_)
            self.R.tag = tag
            n = job["n"]
            ps_, psk = self.psum()
            for i, c in enumerate(chunks):
                m = job["masks"][c] if job["masks"] else None
                self.mm(ps_[:, i * n:(i + 1) * n], job["kch"][c][0], job["qa"], True, m is None,
                        [job["kch"][c][1], job["qak"]], [psk])
                if m is not None:
                    self.mm(ps_[:, i * n:(i + 1) * n], self.identb[:], m[0], False, True, ["identb", m[1]], [psk])
            pt, ptk = self.b16()
            w = len(chunks) * n
            self.act(pt[:, 0:w], ps_[:, 0:w], ACTF.Exp, [psk], [ptk])
            pend.append((job, chunks, pt, ptk))
            if len(pend) > look:
                emit_pv(pend.pop(0))
        while pend:
            emit_pv(pend.pop(0))
        if bg:
            for g_ in bg:
                for _ in g_:
                    pass

    def attn_finish_gen(self, po, pok, n, dst, dkey, tag=""):
        yield
        yield
        self.R.tag = tag
        self.act(self.rsc[64:65, 0:n], po[64:65, 0:n], ACTF.Ln, [pok], ["rsc"])
        self.act(self.rsc[64:65, 0:n], self.rsc[64:65, 0:n], ACTF.Exp, ["rsc"], ["rsc"], scale=-1.0)
        o = self.osb
        self.copy(o[0:64, 0:n], po[0:64, 0:n], [pok], ["osb"])
        yield
        self.R.tag = tag
        pb, pbk = self.psum()
        self.mm(pb[0:64, 0:n], self.esel[:], self.rsc[0:65, 0:n], True, True, ["rsc", "esel"], [pbk])
        yield
        self.R.tag = tag
        self.tt(dst, o[0:64, 0:n], pb[0:64, 0:n], ALU.mult, ["osb", pbk], [dkey])

    def phase_c(self, l, t, V, last):
        d = self.din
        self.R.tag = "c_norm"
        sample = t < 4
        cv = 0 if sample else 1
        KaT, Va, KcT, Vc = V["KaT"], V["Va"], V["KcT"], V["Vc"]
        oaT, ocT = V["oaT"], V["ocT"]
        self.load_x(l, t)
        self.norm_mod(self.xt, "xt", self.a1, 0, cv, self.hT, "hT")
        if sample:
            self.load_rope(V, t)
        self.R.tag = "c_attnA"
        wqa, wqak = self.wload(d["wqa%d" % l][0], 4096)
        wqc, wqck = self.wload(d["wqc%d" % l][0], 2048)
        acc = [0]

        P4, P5 = self.ps[4], self.ps[5]

        def prep_gen(kind, h):
            self.R.tag = "c_attn%s.prep" % kind.upper()
            qa = V["qaug"][h % 2]
            qak = "qaug%d" % (h % 2)
            if kind == "a":
                g = h // 4
                for k in range(KC):
                    self.mm(P4[0:64, :], wqa[:, k * 512 + h * 64:k * 512 + h * 64 + 64], self.hT[:, k, :], k == 0, k == KC - 1,
                            [wqak, "hT"], ["ps4"])
                yield
                sq, sk = self.b16()
                pc4, pc4k = self.f32()
                self.copy(pc4[0:64, :], P4[0:64, :], ["ps4"], [pc4k])
                self.tt(sq[0:64, :], pc4[0:64, :], pc4[0:64, :], ALU.mult, [pc4k], [sk])
                self.mm(P5[0:64, :], self.onesb[0:64, 0:64], sq[0:64, :], True, True, [sk, "onesb"], ["ps5"])
                yield
                r, rk = self.f32()
                self.act(r[0:64, :], P5[0:64, :], ACTF.Ln, ["ps5"], [rk], bias=self.epsb[0:64, :], scale=1.0 / 64)
                self.act(r[0:64, :], r[0:64, :], ACTF.Exp, [rk], [rk], scale=-0.5)
                qn, qnk = self.f32()
                self.stt(qn[0:64, :], P4[0:64, :], self.small[0:64, 74:75], r[0:64, :], ALU.mult, ALU.mult, ["ps4", rk, "small"], [qnk])
                yield
                if sample:
                    self.mm(P5[0:64, :], self.prot[:], qn[0:64, :], True, True, [qnk, "prot"], ["ps5"])
                    t1, k1 = self.f32()
                    self.tt(t1[0:64, :], qn[0:64, :], V["rope"][:, 0, :], ALU.mult, [qnk, "rope"], [k1])
                    yield
                    t2, k2 = self.f32()
                    self.tt(t2[0:64, :], P5[0:64, :], V["rope"][:, 1, :], ALU.mult, ["ps5", "rope"], [k2])
                    self.tt(t1[0:64, :], t1[0:64, :], t2[0:64, :], ALU.add, [k1, k2], [k1])
                    qn, qnk = t1, k1
                    yield
                col = g
                qsrc, qsk = qn[0:64, :], qnk
            else:
                for k in range(KC):
                    self.mm(P4[0:64, :], wqc[:, k * 256 + h * 64:k * 256 + h * 64 + 64], self.hT[:, k, :], k == 0, k == KC - 1,
                            [wqck, "hT"], ["ps4"])
                yield
                col = 2 + h
                qsrc, qsk = P4[0:64, :], "ps4"
            self.ts(qa[0:64, :], qsrc, 0.125, None, ALU.mult, None, [qsk], [qak])
            sq, sk = self.b16()
            if kind == "a":
                self.tt(sq[0:64, :], qsrc, qsrc, ALU.mult, [qsk], [sk])
            else:
                self.act(sq[0:64, :], qsrc, ACTF.Square, [qsk], [sk])
            self.mm(P5[0:65, :], self.eaug[:], sq[0:64, :], True, True, [sk, "eaug"], ["ps5"])
            yield
            r, rk = self.f32()
            self.act(r[64:65, :], P5[64:65, :], ACTF.Ln, ["ps5", "kmax"], [rk], scale=self.kmax[64:65, col:col + 1],
                     bias=self.tinyb[64:65, :])
            self.act(r[64:65, :], r[64:65, :], ACTF.Exp, [rk], [rk], scale=0.5)
            self.ts(qa[64:65, :], r[64:65, :], -0.125, None, ALU.mult, None, [rk], [qak])

        def jobs_a(h, qa, qak, po, pok):
            g = h // 4
            if sample:
                return [dict(qa=qa[0:128, :], qak=qak, n=TT, po=po, pok=pok, ocol=0, masks=None,
                             kch=[(KaT[0:128, g, j * 128:(j + 1) * 128], "KaT") for j in range(18)],
                             vch=[(Va[:, j, :, :].rearrange("p a b -> p (a b)")[:, 66 * g:66 * g + 128], "Va") for j in range(18)])]
            return [dict(qa=qa[0:128, s_ * 256:(s_ + 1) * 256], qak=qak, n=256, po=po, pok=pok, ocol=s_ * 256, masks=None,
                         kch=[(KaT[0:128, g, s_ * 256 + j * 128:s_ * 256 + (j + 1) * 128], "KaT") for j in range(2)],
                         vch=[(Va[:, 2 * s_ + j, :, :].rearrange("p a b -> p (a b)")[:, 66 * g:66 * g + 128], "Va") for j in range(2)]) for s_ in range(2)]

        def jobs_c(h, qa, qak, po, pok):
            if not sample:
                return [dict(qa=qa[0:128, s_ * 256:(s_ + 1) * 256], qak=qak, n=256, po=po, pok=pok, ocol=s_ * 256, masks=None,
                             kch=[(KcT[0:128, h, s_ * 256 + j * 128:s_ * 256 + (j + 1) * 128], "KcT") for j in range(2)],
                             vch=[(Vc[:, 2 * s_ + j, :, :].rearrange("p a b -> p (a b)")[:, 66 * h:66 * h + 128], "Vc") for j in range(2)]) for s_ in range(2)]
            jobs = []
            for b4 in range(4):
                b = 4 * t + b4
                pi = {0: 0, 1: 1, 14: 3, 15: 4}.get(b, 2)
                R0 = min(max(2 * b - 4, 0), 22)
                if pi == 2:
                    mb, mbk = V["mgen"], "mgen"
                else:
                    mb, mbk = V["mbrd"][pi % 2], "mbrd%d" % (pi % 2)
                kch = [(KcT[0:128, h, j * 128:(j + 1) * 128], "KcT") for j in range(2)]
                vch = [(Vc[:, j, :, :].rearrange("p a b -> p (a b)")[:, 66 * h:66 * h + 128], "Vc") for j in range(2)]
                masks = [None, None]
                for jj in range(5):
                    kc0 = 256 + 64 * R0 + 128 * jj
                    kch.append((KcT[0:128, h, kc0:kc0 + 128], "KcT"))
                    vch.append((Vc[:, 2 + R0 // 2 + jj, :, :].rearrange("p a b -> p (a b)")[:, 66 * h:66 * h + 128], "Vc"))
                    masks.append((mb[:, h, jj, :], mbk))
                jobs.append(dict(qa=qa[0:128, b4 * 128:(b4 + 1) * 128], qak=qak, n=128, po=po, pok=pok, ocol=b4 * 128,
                                 masks=masks, kch=kch, vch=vch))
            return jobs

        if sample and t in (0, 3):
            for pi in ((0, 1) if t == 0 else (3, 4)):
                self.dma(V["mbrd"][pi % 2][:].rearrange("p a b c -> p (a b c)"), self.mt[pi], ["mt%d" % pi],
                         ["mbrd%d" % (pi % 2)], eng="pool", chan="msk")
        heads = [("a", h) for h in range(8)] + [("c", h) for h in range(4)]
        self.psum_n = 4
        for _ in prep_gen(*heads[0]):
            pass
        fin = None
        for i, (kind, h) in enumerate(heads):
            qa = V["qaug"][h % 2]
            qak = "qaug%d" % (h % 2)
            bgs = []
            if fin is not None:
                bgs.append(fin)
            if i + 1 < len(heads):
                bgs.append(prep_gen(*heads[i + 1]))
            po, pok = self.ps[6 + acc[0] % 2], "ps%d" % (6 + acc[0] % 2)
            acc[0] += 1
            ctag = "c_attn%s.core" % kind.upper()
            jobs = jobs_a(h, qa, qak, po, pok) if kind == "a" else jobs_c(h, qa, qak, po, pok)
            self.attn_stream(jobs, bg=bgs, tag=ctag)
            dst = oaT[:, h, :] if kind == "a" else ocT[:, h, :]
            fin = self.attn_finish_gen(po, pok, TT, dst, "oaT" if kind == "a" else "ocT", ctag)
        for _ in fin:
            pass
        self.psum_n = 6
        self.R.tag = "c_merge"
        mg = V["merged"]
        tok = slice(t * TT, (t + 1) * TT)
        for m in range(8):
            wb, wbk = self.wload(d["wbr%d" % l][m], 1792)
            wg, wgk = self.wload(d["wg%d" % l][m], 8 * 384)
            pa, pak = self.psum()
            for h in range(8):
                self.mm(pa[:], wb[0:64, h * 128:(h + 1) * 128], oaT[:, h, :], h == 0, h == 7, [wbk, "oaT"], [pak])
            pb, pbk = self.psum()
            for k in range(2):
                self.mm(pb[:], wb[:, 1536 + k * 128:1536 + (k + 1) * 128], self.obT[:, k, tok], k == 0, k == 1, [wbk, "obT"], [pbk])
            pc, pck = self.psum()
            for h in range(4):
                self.mm(pc[:], wb[0:64, 1024 + h * 128:1024 + (h + 1) * 128], ocT[:, h, :], h == 0, h == 3, [wbk, "ocT"], [pck])
            accum = None
            for bi, (pp, ppk) in enumerate(((pa, pak), (pb, pbk), (pc, pck))):
                pg, pgk = self.psum()
                for k in range(KC):
                    self.mm(pg[:], wg[:, k * 384 + bi * 128:k * 384 + (bi + 1) * 128], self.hT[:, k, :], k == 0, k == KC - 1,
                            [wgk, "hT"], [pgk])
                e, ek = self.f32()
                self.act(e[:], pg[:], ACTF.Tanh, [pgk], [ek], scale=0.5)
                if bi == 0:
                    self.stt(e[:], e[:], 1.0, pp[:], ALU.add, ALU.mult, [ek, ppk], [ek])
                    accum = (e, ek)
                elif bi == 1:
                    self.stt(e[:], e[:], 1.0, pp[:], ALU.add, ALU.mult, [ek, ppk], [ek])
                    self.tt(accum[0][:], accum[0][:], e[:], ALU.add, [accum[1], ek], [accum[1]])
                else:
                    self.stt(e[:], e[:], 1.0, pp[:], ALU.add, ALU.mult, [ek, ppk], [ek])
                    self.tt(mg[:, m, :], accum[0][:], e[:], ALU.add, [accum[1], ek], ["merged"])
        self.R.tag = "c_out"
        for half in range(2):
            wo, wok = self.wload(d["wout%d" % l][half], 4096)
            for mm_ in range(4):
                m = half * 4 + mm_
                p, pk = self.psum()
                for k in range(KC):
                    self.mm(p[:], wo[:, k * 512 + mm_ * 128:k * 512 + (mm_ + 1) * 128], mg[:, k, :], k == 0, k == KC - 1,
                            [wok, "merged"], [pk])
                self.stt(self.xt[:, m, :], p[:], self.gth[:, 0, m, cv:cv + 1], self.xt[:, m, :], ALU.mult, ALU.add, [pk, "gth", "xt%d" % m], ["xt%d" % m])
        self.R.tag = "c_ffn"
        self.norm_mod(self.xt, "xt", self.a2, 3, cv, self.hT, "hT")
        actT = V["actT"]
        for u in range(11):
            w, wk = self.wload(d["wgu%d" % l][u], 4096)
            for ff in range(2):
                f = 2 * u + ff
                pg, pgk = self.psum()
                for k in range(KC):
                    self.mm(pg[:], w[:, k * 512 + ff * 128:k * 512 + (ff + 1) * 128], self.hT[:, k, :], k == 0, k == KC - 1,
                            [wk, "hT"], [pgk])
                pu, puk = self.psum()
                for k in range(KC):
                    self.mm(pu[:], w[:, k * 512 + 256 + ff * 128:k * 512 + 256 + (ff + 1) * 128], self.hT[:, k, :], k == 0, k == KC - 1,
                            [wk, "hT"], [puk])
                e, ek = self.f32()
                self.act(e[:], pg[:], ACTF.Tanh, [pgk], [ek], scale=0.5)
                self.stt(e[:], e[:], 1.0, pg[:], ALU.add, ALU.mult, [ek, pgk], [ek])
                self.tt(actT[:, f, :], e[:], pu[:], ALU.mult, [ek, puk], ["actT"])
        for m in range(8):
            w, wk = self.wload(d["wd%d" % l][m], NF * 128)
            p, pk = self.psum()
            for f in range(NF):
                self.mm(p[:], w[:, f * 128:(f + 1) * 128], actT[:, f, :], f == 0, f == NF - 1, [wk, "actT"], [pk])
            self.stt(self.xt[:, m, :], p[:], self.gth[:, 1, m, cv:cv + 1], self.xt[:, m, :], ALU.mult, ALU.add, [pk, "gth", "xt%d" % m], ["xt%d" % m])
            if not last:
                self.dma(self.xs[t][:, m * TT:(m + 1) * TT], self.xt[:, m, :], ["xt%d" % m], ["xs%d_%d" % (t, m)])
        if last:
            r, rk = self.rstd_of(self.xt, "xt")
            for k in range(KC):
                self.stt(self.xt[:, k, :], self.xt[:, k, :], self.small[:, 64 + k:65 + k], r[:], ALU.mult, ALU.mult,
                         ["xt%d" % k, rk, "small"], ["xt%d" % k])
                self.dma(self.dout["y_out"][t][:, k * TT:(k + 1) * TT], self.xt[:, k, :], ["xt%d" % k], ["o_y%d_%d" % (t, k)])

    def build(self):
        nc = self.nc
        self.epsb = self.sb("epsb", [128, 1], F32)
        self.tinyb = self.sb("tinyb", [128, 1], F32)
        self.memset(self.epsb[:], EPS, ["epsb"])
        self.memset(self.tinyb[:], 1e-30, ["tinyb"])
        self.setup()
        VB = {}
        o = [0]

        def take(Vd, name, shape, dt):
            esz = 4 if dt == F32 else 2
            n = 1
            for s_ in shape[1:]:
                n *= s_
            Vd[name] = self.uview(o[0], shape, dt)
            o[0] += (n * esz + 31) // 32 * 32
        take(VB, "U", [128, 16, NCH], BF16)
        take(VB, "du", [128, 2, NTOK], F32)
        take(VB, "WS", [128, 16, 2, 128], BF16)
        take(VB, "WH", [128, 16, 2, 128], BF16)
        take(VB, "KM", [128, 16, 2, 128], BF16)
        take(VB, "lamp", [128, 16, 8, 3], F32)
        take(VB, "lh0", [128, 16, 2], F32)
        take(VB, "h0", [128, 16, 2], F32)
        take(VB, "nss", [128, 16, 2, 2], F32)
        base = o[0]
        take(VB, "ssmp", [128, 16, 67], F32)
        take(VB, "S", [128, 64, 16], F32)
        take(VB, "bbar", [128, 2, 16, 16], F32)
        take(VB, "tmpA", [128, 8, 8, 16], F32)
        take(VB, "tmpB", [128, 8, 8, 16], F32)
        take(VB, "L", [128, 2, 16, 8], F32)
        take(VB, "cimn", [128, 16, 16], F32)
        take(VB, "H", [128, 2, 4, 2, 128], F32)
        take(VB, "urep0", [128, 16, 8, 16], BF16)
        take(VB, "urep1", [128, 16, 8, 16], BF16)
        VB["urep"] = [VB["urep0"], VB["urep1"]]
        end1 = o[0]
        o[0] = base
        take(VB, "Eprev", [128, 16, 2, NCH], BF16)
        zone2 = o[0]
        take(VB, "E0", [128, 4, 2, NCH], F32)
        take(VB, "E1", [128, 4, 2, NCH], F32)
        end2 = o[0]
        o[0] = zone2
        take(VB, "Ysb", [128, 16, NCH], F32)
        take(VB, "ygf", [128, 2, 512], F32)
        take(VB, "ygb", [128, 2, 512], BF16)
        end3 = o[0]
        o[0] = base
        take(VB, "mqs0", [128, 4, 640], F32)
        take(VB, "mqs1", [128, 4, 640], F32)
        take(VB, "mts0", [128, 4, 5, 128], BF16)
        take(VB, "mts1", [128, 4, 5, 128], BF16)
        VB["mqs"] = [VB["mqs0"], VB["mqs1"]]
        VB["mts"] = [VB["mts0"], VB["mts1"]]
        assert max(end1, end2, end3, o[0]) <= self.uni_bytes, (end1, end2, end3, o[0])
        VC = {}
        o[0] = 0
        take(VC, "KaT", [128, 2, 2304], BF16)
        take(VC, "KcT", [128, 4, 2304], BF16)
        take(VC, "Va", [128, 18, 3, 66], BF16)
        take(VC, "Vc", [128, 18, 5, 66], BF16)
        take(VC, "qaug0", [128, 512], BF16)
        take(VC, "qaug1", [128, 512], BF16)
        VC["qaug"] = [VC["qaug0"], VC["qaug1"]]
        take(VC, "oaT", [64, 8, 512], BF16)
        take(VC, "ocT", [64, 4, 512], BF16)
        take(VC, "rope", [64, 2, 512], F32)
        take(VC, "mgen", [128, 4, 5, 128], BF16)
        take(VC, "mbrd0", [128, 4, 5, 128], BF16)
        take(VC, "mbrd1", [128, 4, 5, 128], BF16)
        VC["mbrd"] = [VC["mbrd0"], VC["mbrd1"]]
        take(VC, "merged", [128, 8, 512], BF16)
        take(VC, "actT", [128, 22, 512], BF16)
        assert o[0] <= self.uni_bytes, o[0]

        import os
        stop = os.environ.get("MK_STOP", "")

        class _Stop(Exception):
            pass

        def chk(tag):
            if stop == tag:
                raise _Stop()
        try:
            chk("setup")
            for l in range(DEPTH):
                last = l == DEPTH - 1
                self.R.pfx = "L%d." % l
                self.layer_params(l)
                chk("params%d" % l)
                self.build_masks(l, VB)
                chk("masks%d" % l)
                self.R.barrier()
                self.ssm_pre(l, VB)
                chk("pre%d" % l)
                self.phase_a1(l, VB)
                chk("a1_%d" % l)
                self.R.barrier()
                self.ssm_main_wrap(l, VB)
                chk("ssm%d" % l)
                self.R.barrier()
                self.init_stores(l, VC, True)
                chk("is%d" % l)
                self.dma(VC["mgen"][:].rearrange("p a b c -> p (a b c)"), self.mt[2], ["mt2"], ["mgen"], eng="pool", chan="msk")
                chk("mg%d" % l)
                for t in range(4):
                    self.R.pfx = "L%d.t%d." % (l, t)
                    self.phase_a2(l, t, VC)
                    chk("a2_%d_%d" % (l, t))
                chk("a2s%d" % l)
                for t in range(4):
                    self.R.pfx = "L%d.t%d." % (l, t)
                    self.phase_c(l, t, VC, last)
                    chk("c%d_%d" % (l, t))
                self.R.pfx = "L%d.t4." % l
                self.init_stores(l, VC, False)
                self.phase_a2(l, 4, VC)
                self.phase_c(l, 4, VC, last)
                chk("layer%d" % l)
                self.R.barrier()
        except _Stop:
            pass
        self.R.emit()
        import os as _os
        if _os.environ.get("MK_NAMES"):
            import json as _json
            _json.dump(self.R.names, open(_os.environ["MK_NAMES"], "w"))
        return nc

    def ssm_main_wrap(self, l, V):
        self.ssm_scan_only = True
        self.ssm_main(l, V)


_PROG_CACHE = {}
LAST_RESULTS = None


def _shapes(shared, percore):
    s = {}
    for k, v in shared.items():
        s[k] = v.shape
    for k, v in percore.items():
        s[k] = v.shape
    return s


def kernel(**inputs):
    inp = {k: np.asarray(v) for k, v in inputs.items()}
    shared = _prep_shared(inp)
    percore = [_prep_core(inp, c) for c in range(8)]
    shapes = _shapes(shared, percore[0])
    prog = Prog(shapes)
    nc = prog.build()
    in_maps = []
    for c in range(8):
        m = dict(shared)
        m.update(percore[c])
        in_maps.append({k: np.ascontiguousarray(v, dtype=np.float32) for k, v in m.items()})
    import os
    ncores = int(os.environ.get("MK_CORES", "8"))
    res = run_bass_kernel_spmd(nc, in_maps[:ncores], core_ids=list(range(ncores)))
    R = res.results
    global LAST_RESULTS
    LAST_RESULTS = R
    B, S = 16, 256
    y_prompt = np.zeros((B, S, D), np.float32)
    y_sample = np.zeros((4, LS, D), np.float32)
    new_ga_k = np.zeros((B, DEPTH, S, 2, 64), np.float32)
    new_ga_v = np.zeros((B, DEPTH, S, 2, 64), np.float32)
    new_na_k = np.zeros((B, DEPTH, S, 4, 64), np.float32)
    new_na_v = np.zeros((B, DEPTH, S, 4, 64), np.float32)
    new_ssm = np.zeros((B, DEPTH, 2, 2, 16, 64), np.float32)
    for c in range(ncores):
        r = R[c]
        yo = np.asarray(r["y_out"]).reshape(NT, 128, KC, TT)
        ytm = yo.transpose(0, 3, 2, 1).reshape(NT, TT, D)
        if c % 2 == 0:
            y_sample[c // 2] = ytm[0:4].reshape(LS, D)
        y_prompt[2 * c] = ytm[4, 0:256]
        y_prompt[2 * c + 1] = ytm[4, 256:512]
        nka = np.asarray(r["nk_a"]).reshape(DEPTH, 64, 2, 2, 256)
        nva = np.asarray(r["nv_a"]).reshape(DEPTH, 128, 2, 2, 2, 64)
        nkc = np.asarray(r["nk_c"]).reshape(DEPTH, 64, 4, 2, 256)
        nvc = np.asarray(r["nv_c"]).reshape(DEPTH, 128, 2, 2, 4, 64)
        nss = np.asarray(r["nssm"]).reshape(DEPTH, 2, 64, 2, 8, 2, 2)
        for s_ in range(2):
            bb = 2 * c + s_
            new_ga_k[bb] = nka[:, :, :, s_, :].transpose(0, 3, 2, 1)
            new_na_k[bb] = nkc[:, :, :, s_, :].transpose(0, 3, 2, 1)
            new_ga_v[bb] = nva[:, :, s_].transpose(0, 2, 1, 3, 4).reshape(DEPTH, 256, 2, 64)
            new_na_v[bb] = nvc[:, :, s_].transpose(0, 2, 1, 3, 4).reshape(DEPTH, 256, 4, 64)
            x = nss[:, :, :, :, :, s_, :]
            new_ssm[bb] = x.transpose(0, 3, 5, 4, 1, 2).reshape(DEPTH, 2, 2, 16, 64)
    return (y_prompt, y_sample, new_ga_k, new_ga_v, new_na_k, new_na_v, new_ssm)
```

```python
import math
from contextlib import ExitStack
import numpy as np
import concourse.bass as bass
import concourse.mybir as mybir
from concourse.bass_utils import run_bass_kernel_spmd

F32 = mybir.dt.float32
BF16 = mybir.dt.bfloat16
ALU = mybir.AluOpType
ACTF = mybir.ActivationFunctionType
AX = mybir.AxisListType

D = 1024
KC = 8
TT = 512
NT = 5
LS = 2048
LP = 256
NTOK = 2560
DFF = 2816
NF = 22
DEPTH = 2
EPS = 1e-6
NEG = -30000.0
NCH = 320
GELU_C = 0.7978845608028654
NSLOT = 4
SLOT = 4096

COMPUTE = ("pe", "act", "dve", "pool")


class Op:
    __slots__ = ("eng", "fn", "reads", "writes", "dma", "idx", "deps", "sig", "ticket", "chan", "chan_n", "tag")

    def __init__(self, eng, fn, reads, writes, dma):
        self.eng = eng
        self.fn = fn
        self.reads = reads
        self.writes = writes
        self.dma = dma
        self.deps = []
        self.sig = False
        self.ticket = 0
        self.chan = None
        self.chan_n = 0


class Rec:
    def __init__(self, nc, n_chan=12):
        self.nc = nc
        self.ops = []
        self.last_w = {}
        self.readers = {}
        self.floor = {}
        self.n_chan = n_chan
        self.chan_rr = 0
        self.chan_last = {}
        self.tag = ""
        self.pfx = ""
        self.names = {}

    def op(self, eng, fn, reads=(), writes=(), dma=False, chan=None):
        ps_reads = [k for k in reads if k.startswith("ps")]
        if ps_reads:
            reads = [k for k in reads if not k.startswith("ps")]
            writes = list(writes) + [k for k in ps_reads if k not in writes]
        o = Op(eng, fn, tuple(reads), tuple(writes), dma)
        o.idx = len(self.ops)
        o.tag = self.pfx + self.tag
        deps = set()
        for k in o.reads:
            w = self.last_w.get(k)
            if w is not None:
                deps.add(w)
        for k in o.writes:
            w = self.last_w.get(k)
            if w is not None:
                deps.add(w)
            for r in self.readers.get(k, ()):
                deps.add(r)
        for f in self.floor.values():
            deps.add(f)
        if dma:
            if chan is None:
                chan = self.chan_rr
                self.chan_rr = (self.chan_rr + 1) % self.n_chan
            o.chan = chan
            prev = self.chan_last.get(chan)
            if prev is not None:
                deps.add(prev)
            self.chan_last[chan] = o
        deps.discard(o)
        best = {}
        keep = []
        for d_ in deps:
            if d_.dma:
                keep.append(d_)
            else:
                b_ = best.get(d_.eng)
                if b_ is None or d_.idx > b_.idx:
                    best[d_.eng] = d_
        o.deps = keep + list(best.values())
        for k in o.reads:
            self.readers.setdefault(k, []).append(o)
        for k in o.writes:
            self.last_w[k] = o
            self.readers[k] = []
        self.ops.append(o)
        return o

    def barrier(self):
        last = {}
        for o in self.ops:
            key = ("dma", o.chan) if o.dma else o.eng
            last[key] = o
        self.floor = last
        self.last_w = {}
        self.readers = {}

    def emit(self):
        nc = self.nc
        ops = self.ops
        for o in ops:
            for d in o.deps:
                if d.dma:
                    continue
                if d.eng == "pe" and o.eng == "pe" and not o.dma:
                    continue
                d.sig = True
        cnt = {e: 0 for e in COMPUTE}
        chan_cnt = {}
        for o in ops:
            if o.dma:
                chan_cnt[o.chan] = chan_cnt.get(o.chan, 0) + 1
                o.chan_n = chan_cnt[o.chan]
            elif o.sig:
                cnt[o.eng] += 1
                o.ticket = cnt[o.eng]
        with ExitStack() as st:
            sems = {e: st.enter_context(nc.semaphore("s_" + e)) for e in COMPUTE}
            csems = {c: st.enter_context(nc.semaphore("c_%s" % str(c))) for c in sorted(chan_cnt, key=str)}
            block = st.enter_context(nc.Block())
            engs = {"pe": "tensor", "act": "scalar", "dve": "vector", "pool": "gpsimd", "sp": "sync"}

            def run(ename):
                def body(eng):
                    seen = {}
                    for o in ops:
                        if o.eng != ename:
                            continue
                        need = {}
                        for d in o.deps:
                            if d.dma:
                                s = csems[d.chan]
                                v = 16 * d.chan_n
                            else:
                                if d.eng == "pe" and ename == "pe" and not o.dma:
                                    continue
                                s = sems[d.eng]
                                v = d.ticket
                            key = id(s)
                            if v > need.get(key, (None, 0))[1]:
                                need[key] = (s, v)
                        for key, (s, v) in need.items():
                            if seen.get(key, 0) >= v:
                                continue
                            eng.wait_ge(s, v)
                            seen[key] = v
                        ins = o.fn(eng)
                        if self.names is not None:
                            try:
                                self.names[ins.ins.name] = o.tag
                            except Exception:
                                pass
                        if o.dma:
                            ins.then_inc(csems[o.chan], 16)
                        elif o.sig:
                            ins.then_inc(sems[ename], 1)
                    if ename == "sp":
                        for c, n in chan_cnt.items():
                            eng.wait_ge(csems[c], 16 * n)
                        for e in COMPUTE:
                            if cnt[e] > 0:
                                eng.wait_ge(sems[e], cnt[e])
                return body

            for ename, attr in engs.items():
                if ename == "sp" or any(o.eng == ename for o in ops):
                    getattr(block, attr)(run(ename))
        return cnt, chan_cnt


def _fm(x):
    T, F = x.shape
    return np.ascontiguousarray(x.reshape(T, F // 128, 128).transpose(2, 1, 0))


def _wunit(w):
    K, C = w.shape
    return np.ascontiguousarray(w.reshape(K // 128, 128, C).transpose(1, 0, 2))


def _consts():
    c = {}
    c["ident"] = np.eye(128, dtype=np.float32)
    P = np.zeros((64, 64), np.float32)
    for base in (0, 32):
        for i in range(16):
            P[base + 16 + i, base + i] = -1.0
            P[base + i, base + 16 + i] = 1.0
    c["prot"] = P
    t = np.arange(LS)
    row = (t // 64).astype(np.float32)
    col = (t % 64).astype(np.float32)
    inv = (1.0 / (10000.0 ** (np.arange(16, dtype=np.float32) / 16.0))).astype(np.float32)
    ang = np.zeros((64, LS), np.float32)
    for d in range(64):
        pos = row if d < 32 else col
        ang[d] = pos * inv[d % 16]
    c["ropec"] = np.cos(ang).astype(np.float32)
    c["ropes"] = np.sin(ang).astype(np.float32)
    i = np.arange(128)
    mf = np.zeros((128, 8), np.float32)
    mf[i, i % 8] = 1.0
    c["mfwd"] = mf
    sf = np.zeros((128, 16), np.float32)
    sf[i, i // 8] = 1.0
    c["selfwd"] = sf
    jm = np.zeros((128, 8), np.float32)
    jm[i, i // 16] = 1.0
    c["jmask"] = jm
    rs = np.zeros((128, 8, 128), np.float32)
    for g in range(8):
        for p in range(128):
            rs[p, g, g * 16 + (p % 16)] = 1.0
    c["revsel"] = rs
    c["negfill"] = np.full((128, 2560), NEG, np.float32)
    return c


def _prep_core(inp, core):
    b = core // 2
    p0, p1 = 2 * core, 2 * core + 1
    m = {}
    xs = inp["x_sample"][b]
    xp = np.concatenate([inp["x_prompt"][p0], inp["x_prompt"][p1]], axis=0)
    xin = np.stack([_fm(xs[512 * t:512 * t + 512]) for t in range(4)] + [_fm(xp)], axis=0)
    m["xin"] = xin.reshape(NT, 128, KC * TT)
    cv = np.stack([inp["c"][b], inp["c_ctx"]], axis=1)
    m["cvT"] = np.ascontiguousarray(cv.reshape(8, 128, 2).transpose(1, 0, 2)).reshape(128, 16)
    for l in range(DEPTH):
        ck = inp["cache_ga_k"][b, l]
        m["ckaT%d" % l] = np.ascontiguousarray(ck.transpose(2, 1, 0)).reshape(64, 2 * 256)
        cvv = inp["cache_ga_v"][b, l]
        m["cva%d" % l] = np.ascontiguousarray(cvv.reshape(2, 128, 2, 64).transpose(1, 0, 2, 3)).reshape(128, 256)
        ck = inp["cache_na_k"][b, l]
        m["ckcT%d" % l] = np.ascontiguousarray(ck.transpose(2, 1, 0)).reshape(64, 4 * 256)
        cvv = inp["cache_na_v"][b, l]
        m["cvc%d" % l] = np.ascontiguousarray(cvv.reshape(2, 128, 4, 64).transpose(1, 0, 2, 3)).reshape(128, 512)
        st = inp["state_ssm"][b, l]
        h0 = np.zeros((128, 16, 2), np.float32)
        for d in range(2):
            for gp in range(8):
                for gl in range(2):
                    h0[gl * 64:(gl + 1) * 64, d * 8 + gp, :] = st[d, :, 2 * gp + gl, :].T
        m["h0_%d" % l] = h0.reshape(128, 32)
    return m


def _prep_shared(inp):
    m = {}
    c = _consts()
    for k, v in c.items():
        m["c_" + k] = np.ascontiguousarray(v.reshape(v.shape[0], -1))
    for l in range(DEPTH):
        wm = _wunit(inp["w_mod"][l])
        m["wmod%d" % l] = np.ascontiguousarray(
            wm.reshape(128, 8, 12, 512).transpose(2, 0, 1, 3)).reshape(12, 128, 4096)
        win = _wunit(inp["w_in"][l])
        qa, ka, va, u = win[:, :, 0:512], win[:, :, 512:640], win[:, :, 640:768], win[:, :, 768:1024]
        qc, kc, vc = win[:, :, 1024:1280], win[:, :, 1280:1536], win[:, :, 1536:1792]
        g = win[:, :, 1792:]
        m["wu%d" % l] = np.ascontiguousarray(u).reshape(1, 128, 2048)
        m["wkv%d" % l] = np.ascontiguousarray(np.concatenate([ka, va, vc], axis=2)).reshape(1, 128, 4096)
        m["wkc%d" % l] = np.ascontiguousarray(kc).reshape(1, 128, 2048)
        m["wqa%d" % l] = np.ascontiguousarray(qa).reshape(1, 128, 4096)
        m["wqc%d" % l] = np.ascontiguousarray(qc).reshape(1, 128, 2048)
        g4 = g.reshape(128, 8, 3, 8, 128)
        m["wg%d" % l] = np.ascontiguousarray(g4.transpose(3, 0, 1, 2, 4)).reshape(8, 128, 8 * 384)
        wa = inp["w_br_a"][l].reshape(8, 64, 8, 128)
        wb = inp["w_br_b"][l].reshape(2, 128, 8, 128)
        wc = inp["w_br_c"][l].reshape(4, 64, 8, 128)
        br = np.zeros((8, 128, 1792), np.float32)
        for mm in range(8):
            br[mm, 0:64, 0:1024] = wa[:, :, mm, :].transpose(1, 0, 2).reshape(64, 1024)
            br[mm, 0:64, 1024:1536] = wc[:, :, mm, :].transpose(1, 0, 2).reshape(64, 512)
            br[mm, :, 1536:1792] = wb[:, :, mm, :].transpose(1, 0, 2).reshape(128, 256)
        m["wbr%d" % l] = br
        wo = _wunit(inp["w_out"][l])
        m["wout%d" % l] = np.ascontiguousarray(wo.reshape(128, 8, 2, 512).transpose(2, 0, 1, 3)).reshape(2, 128, 4096)
        wgu = _wunit(inp["w_ffn_gu"][l])
        gate = wgu[:, :, 0:DFF].reshape(128, 8, 11, 256)
        up = wgu[:, :, DFF:].reshape(128, 8, 11, 256)
        m["wgu%d" % l] = np.ascontiguousarray(
            np.concatenate([gate, up], axis=3).transpose(2, 0, 1, 3)).reshape(11, 128, 4096)
        wd = _wunit(inp["w_ffn_d"][l])
        m["wd%d" % l] = np.ascontiguousarray(wd.reshape(128, 22, 8, 128).transpose(2, 0, 1, 3)).reshape(8, 128, 22 * 128)
        m["wglu%d" % l] = _wunit(inp["ssm_w_glu"][l]).reshape(1, 128, 512)
        sm = np.zeros((128, 80), np.float32)
        sm[:, 0:48] = inp["b_mod"][l].reshape(48, 128).T
        sm[:, 48:56] = inp["norm1_g"][l].reshape(8, 128).T
        sm[:, 56:64] = inp["norm2_g"][l].reshape(8, 128).T
        sm[:, 64:72] = inp["final_g"].reshape(8, 128).T
        sm[:, 72:74] = inp["ssm_d"][l].reshape(2, 128).T
        sm[0:64, 74] = inp["qn_g"][l]
        sm[0:64, 75] = inp["kn_g"][l]
        m["small%d" % l] = sm
        sp = np.zeros((128, 16, 67), np.float32)
        for d in range(2):
            for gp in range(8):
                for gl in range(2):
                    g_ = 2 * gp + gl
                    rows = slice(gl * 64, gl * 64 + 64)
                    td = d * 8 + gp
                    sp[rows, td, 0] = inp["ssm_lam_re"][l, d, g_]
                    sp[rows, td, 1] = inp["ssm_lam_im"][l, d, g_]
                    sp[rows, td, 2] = inp["ssm_log_step"][l, d, g_]
                    sp[rows, td, 3:19] = inp["ssm_b_re"][l, d, g_]
                    sp[rows, td, 19:35] = inp["ssm_b_im"][l, d, g_]
                    sp[rows, td, 35:51] = inp["ssm_c_re"][l, d, g_].T
                    sp[rows, td, 51:67] = inp["ssm_c_im"][l, d, g_].T
        m["ssmp%d" % l] = sp.reshape(128, 16 * 67)
        m["nab%d" % l] = np.ascontiguousarray(inp["na_bias"][l]).reshape(4, 465)
    return m


IN_SHAPES = None


class Prog:
    def __init__(self, shapes):
        self.nc = nc = bass.Bass("TRN2", target_bir_lowering=False)
        self.R = Rec(nc)
        self.din = {}
        for name, shp in shapes.items():
            self.din[name] = nc.dram_tensor(name, list(shp), F32, kind="ExternalInput").ap()
        self.dout = {}

        def out(name, shp):
            self.dout[name] = nc.dram_tensor(name, list(shp), F32, kind="ExternalOutput").ap()
        out("y_out", [NT, 128, KC * TT])
        out("nk_a", [DEPTH, 64, 2 * 512])
        out("nv_a", [DEPTH, 128, 4 * 128])
        out("nk_c", [DEPTH, 64, 4 * 512])
        out("nv_c", [DEPTH, 128, 4 * 256])
        out("nssm", [DEPTH, 128, 16 * 4])
        self.xs = nc.dram_tensor("xs_scr", [NT, 128, KC * TT], F32).ap()
        self.mq = nc.dram_tensor("mq_scr", [5, 128, 4 * 640], F32).ap()
        self.mt = nc.dram_tensor("mt_scr", [5, 128, 2560], BF16).ap()
        self.uid = 0
        self.wi = 0
        self.alloc()

    def sb(self, name, shape, dt):
        return self.nc.alloc_sbuf_tensor("sb_" + name, list(shape), dt)

    def alloc(self):
        nc = self.nc
        self.ps = [nc.alloc_psum_tensor("ps%d" % i, [128, 512], F32) for i in range(8)]
        self.psi = 0
        self.psum_n = 6
        self.wring = [self.sb("wr%d" % i, [128, SLOT], BF16) for i in range(NSLOT)]
        self.f32r = [self.sb("fr%d" % i, [128, 512], F32) for i in range(8)]
        self.f32i = 0
        self.b16r = [self.sb("br%d" % i, [128, 512], BF16) for i in range(6)]
        self.b16i = 0
        self.xt = self.sb("xt", [128, KC, TT], F32)
        self.hT = self.sb("hT", [128, KC, TT], BF16)
        self.obT = self.sb("obT", [128, 2, NTOK], BF16)
        self.identf = self.sb("identf", [128, 128], F32)
        self.identb = self.sb("identb", [128, 128], BF16)
        self.onesb = self.sb("onesb", [128, 128], BF16)
        self.eaug = self.sb("eaug", [64, 65], BF16)
        self.esel = self.sb("esel", [65, 64], F32)
        self.prot = self.sb("prot", [64, 64], F32)
        self.mfwd = self.sb("mfwd", [128, 8], F32)
        self.selfwd = self.sb("selfwd", [128, 16], BF16)
        self.jmask = self.sb("jmask", [128, 8], F32)
        self.revsel = self.sb("revsel", [128, 8, 128], BF16)
        self.rsc = self.sb("rsc", [65, 512], F32)
        self.osb = self.sb("osb", [64, 512], F32)
        self.pscr = [self.sb("pscr%d" % i, [128, 2, 31], F32) for i in range(2)]
        self.cvT = self.sb("cvT", [128, 8, 2], F32)
        self.silT = self.sb("silT", [128, 8, 2], BF16)
        self.small = self.sb("small", [128, 80], F32)
        self.modv = self.sb("modv", [128, 48, 2], F32)
        self.a1 = self.sb("a1", [128, 8, 2], F32)
        self.a2 = self.sb("a2", [128, 8, 2], F32)
        self.gth = self.sb("gth", [128, 2, 8, 2], F32)
        self.kmax = self.sb("kmax", [65, 8], F32)
        self.ktmp = self.sb("ktmp", [65, 8], F32)
        self.wglu = self.sb("wglu", [128, 2, 256], BF16)
        UNI_BYTES = 109 * 1024
        self.uni = self.sb("uni", [128, UNI_BYTES // 2], BF16)
        self.uni_bytes = UNI_BYTES

    def uview(self, off_bytes, shape, dt, parts=128):
        esz = 4 if dt == F32 else 2
        n = 1
        for s in shape[1:]:
            n *= s
        assert off_bytes % 4 == 0 and off_bytes + n * esz <= self.uni_bytes, (off_bytes, n, esz)
        a = self.uni[0:shape[0], off_bytes // 2: off_bytes // 2 + n * esz // 2]
        if dt == F32:
            a = a.bitcast(F32)
        if len(shape) == 2:
            return a
        if len(shape) == 3:
            return a.rearrange("p (a b) -> p a b", a=shape[1])
        if len(shape) == 4:
            return a.rearrange("p (a b c) -> p a b c", a=shape[1], b=shape[2])
        if len(shape) == 5:
            return a.rearrange("p (a b c d) -> p a b c d", a=shape[1], b=shape[2], c=shape[3])
        raise ValueError

    def f32(self):
        i = self.f32i % len(self.f32r)
        self.f32i += 1
        return self.f32r[i], "fr%d" % i

    def b16(self):
        i = self.b16i % len(self.b16r)
        self.b16i += 1
        return self.b16r[i], "br%d" % i

    def psum(self):
        i = self.psi % self.psum_n
        self.psi += 1
        return self.ps[i], "ps%d" % i

    def mm(self, out, lhsT, rhs, start, stop, reads, writes):
        self.R.op("pe", lambda e: e.matmul(out, lhsT, rhs, start=start, stop=stop), reads, writes)

    def tr(self, out, in_, ident, reads, writes):
        self.R.op("pe", lambda e: e.transpose(out, in_, ident), reads, writes)

    def act(self, out, in_, func, reads, writes, bias=None, scale=None):
        kw = {}
        if bias is not None:
            kw["bias"] = bias
        if scale is not None:
            kw["scale"] = scale
        self.R.op("act", lambda e: e.activation(out=out, in_=in_, func=func, **kw), reads, writes)

    def tt(self, out, a, b, op, reads, writes, eng="dve"):
        self.R.op(eng, lambda e: e.tensor_tensor(out, a, b, op), reads, writes)

    def ts(self, out, a, s1, s2, op0, op1, reads, writes, eng="dve"):
        if s2 is None:
            self.R.op(eng, lambda e: e.tensor_scalar(out, a, s1, None, op0), reads, writes)
        else:
            self.R.op(eng, lambda e: e.tensor_scalar(out, a, s1, s2, op0, op1), reads, writes)

    def stt(self, out, a, s, b, op0, op1, reads, writes):
        self.R.op("dve", lambda e: e.scalar_tensor_tensor(out, a, s, b, op0, op1), reads, writes)

    def recip(self, out, a, reads, writes):
        self.R.op("dve", lambda e: e.reciprocal(out, a), reads, writes)

    def copy(self, out, a, reads, writes, eng="dve"):
        if eng == "act":
            self.R.op("act", lambda e: e.activation(out=out, in_=a, func=ACTF.Copy), reads, writes)
        else:
            self.R.op(eng, lambda e: e.tensor_copy(out, a), reads, writes)

    def memset(self, ap, v, writes, eng="dve"):
        self.R.op(eng, lambda e: e.memset(ap, v), (), writes)

    def dma(self, out, in_, reads, writes, eng="sp", chan=None):
        self.R.op(eng, lambda e: e.dma_start(out=out, in_=in_), reads, writes, dma=True, chan=chan)

    def dbg(self, name, ap2d, key):
        shp = list(ap2d.shape)
        dt = ap2d.dtype
        t = self.nc.dram_tensor("dbg_" + name, shp, dt, kind="ExternalOutput").ap()
        self.dout["dbg_" + name] = t
        self.dma(t, ap2d, [key], ["dbg_" + name])

    def sub(self, i):
        import os
        return int(os.environ.get("MK_SUB", "0")) == i

    def wload(self, src, n):
        s = self.wi % NSLOT
        self.wi += 1
        t = self.wring[s]
        key = "w%d" % s
        self.dma(t[:, 0:n], src, (), [key], eng="pool", chan="w%d" % s)
        return t, key

    def setup(self):
        d = self.din
        ld = self.dma
        fr, fk = self.f32()
        ld(self.identf[:], d["c_ident"], (), ["identf"])
        ld(self.identb[:], d["c_ident"], (), ["identb"], eng="pool", chan="cst")
        self.memset(self.onesb[:], 1.0, ["onesb"])
        self.memset(self.eaug[:], 0.0, ["eaug"])
        self.memset(self.eaug[0:64, 64:65], 1.0, ["eaug"])
        self.memset(self.esel[:], 0.0, ["esel"])
        self.memset(self.esel[64:65, :], 1.0, ["esel"])
        ld(self.prot[:], d["c_prot"], (), ["prot"])
        ld(self.mfwd[:], d["c_mfwd"], (), ["mfwd"])
        ld(self.selfwd[:], d["c_selfwd"], (), ["selfwd"], eng="pool", chan="cst")
        ld(self.jmask[:], d["c_jmask"], (), ["jmask"])
        ld(self.revsel[:].rearrange("p a b -> p (a b)"), d["c_revsel"], (), ["revsel"], eng="pool", chan="cst")
        self.memset(self.rsc[:], 0.0, ["rsc"])
        ld(self.cvT[:].rearrange("p a b -> p (a b)"), d["cvT"], (), ["cvT"])
        cv = self.cvT[:].rearrange("p a b -> p (a b)")
        t1, k1 = self.f32()
        self.act(t1[:, 0:16], cv, ACTF.Exp, ["cvT"], [k1], scale=-1.0)
        self.ts(t1[:, 0:16], t1[:, 0:16], 1.0, None, ALU.add, None, [k1], [k1])
        self.recip(t1[:, 0:16], t1[:, 0:16], [k1], [k1])
        self.tt(self.silT[:].rearrange("p a b -> p (a b)"), t1[:, 0:16], cv, ALU.mult, [k1, "cvT"], ["silT"])

    def layer_params(self, l):
        d = self.din
        self.R.tag = "params"
        self.dma(self.small[:], d["small%d" % l], (), ["small"])
        self.dma(self.wglu[:].rearrange("p a b -> p (a b)"), d["wglu%d" % l][0], (), ["wglu"], eng="pool", chan="cst")
        pm = self.ps[7]
        for u in range(12):
            w, wk = self.wload(d["wmod%d" % l][u], 4096)
            for mm_ in range(4):
                m = 4 * u + mm_
                for k in range(KC):
                    self.mm(pm[:, 2 * m:2 * m + 2], w[:, k * 512 + mm_ * 128: k * 512 + mm_ * 128 + 128],
                            self.silT[:, k, :], k == 0, k == KC - 1, [wk, "silT"], ["ps7"])
        bm = self.small[:, 0:48].unsqueeze(2).to_broadcast([128, 48, 2])
        self.tt(self.modv[:], pm[:, 0:96].rearrange("p (a b) -> p a b", b=2), bm, ALU.add, ["ps7", "small"], ["modv"])
        self.ts(self.gth[:, 0, :, :], self.modv[:, 16:24, :], 0.5, None, ALU.mult, None, ["modv"], ["gth"])
        self.ts(self.gth[:, 1, :, :], self.modv[:, 40:48, :], 0.5, None, ALU.mult, None, ["modv"], ["gth"])
        for (a, an, which, goff) in ((self.a1, "a1", 1, 48), (self.a2, "a2", 4, 56)):
            g = self.small[:, goff:goff + 8].unsqueeze(2).to_broadcast([128, 8, 2])
            self.ts(a[:], self.modv[:, which * 8:which * 8 + 8, :], 1.0, None, ALU.add, None, ["modv"], [an])
            self.tt(a[:], a[:], g, ALU.mult, [an, "small"], [an])

    def modap(self, which, k, cv):
        return self.modv[:, which * 8 + k, cv:cv + 1]

    def rstd_of(self, xt, xkey):
        pst, pk = self.psum()
        for k in range(KC):
            sq, sk = self.b16()
            self.act(sq[:], xt[:, k, :], ACTF.Square, [xkey + str(k)], [sk])
            self.mm(pst[:], self.onesb[:], sq[:], k == 0, k == KC - 1, [sk, "onesb"], [pk])
        r, rk = self.f32()
        self.act(r[:], pst[:], ACTF.Ln, [pk], [rk], bias=self.epsb[:], scale=1.0 / D)
        self.act(r[:], r[:], ACTF.Exp, [rk], [rk], scale=-0.5)
        return r, rk

    def norm_mod(self, xt, xkey, a, which_b, cv, out, okey):
        r, rk = self.rstd_of(xt, xkey)
        an = "a1" if a is self.a1 else "a2"
        for k in range(KC):
            t, tk = self.f32()
            self.stt(t[:], xt[:, k, :], a[:, k, cv:cv + 1], r[:], ALU.mult, ALU.mult, [xkey + str(k), rk, an], [tk])
            self.act(out[:, k, :], t[:], ACTF.Identity, [tk, "modv"], [okey], bias=self.modap(which_b, k, cv))

    def load_x(self, l, t):
        src = self.din["xin"][t] if l == 0 else self.xs[t]
        for k in range(KC):
            self.dma(self.xt[:, k, :], src[:, k * TT:(k + 1) * TT], ["xs%d_%d" % (t, k)], ["xt%d" % k])

    def phase_a1(self, l, V):
        d = self.din
        self.R.tag = "a1"
        for t in range(NT):
            cv = 0 if t < 4 else 1
            self.load_x(l, t)
            self.norm_mod(self.xt, "xt", self.a1, 0, cv, self.hT, "hT")
            w, wk = self.wload(d["wu%d" % l][0], 2048)
            for cc in range(2):
                p, pk = self.psum()
                for k in range(KC):
                    self.mm(p[:], w[:, k * 256 + cc * 128: k * 256 + cc * 128 + 128], self.hT[:, k, :],
                            k == 0, k == KC - 1, [wk, "hT"], [pk])
                self.ts(V["du"][:, cc, t * TT:(t + 1) * TT], p[:], self.small[:, 72 + cc:73 + cc], None, ALU.mult, None,
                        [pk, "small"], ["du"])
            for blk in range(4):
                p, pk = self.psum()
                for k in range(KC):
                    self.mm(p[:, 0:256], self.hT[:, k, blk * 128:(blk + 1) * 128], w[:, k * 256:(k + 1) * 256],
                            k == 0, k == KC - 1, [wk, "hT"], [pk])
                ur = V["urep"][blk % 2]
                uk = "urep%d" % (blk % 2)
                in0 = p[:, 0:256].rearrange("p (g c) -> p g c", g=16).unsqueeze(2).to_broadcast([128, 16, 8, 16])
                in1 = self.mfwd[:].unsqueeze(1).unsqueeze(3).to_broadcast([128, 16, 8, 16])
                self.tt(ur[:], in0, in1, ALU.mult, [pk, "mfwd"], [uk])
                p2, pk2 = self.psum()
                for g in range(16):
                    self.mm(p2[:, g * 16:(g + 1) * 16], ur[:, g, :, :].rearrange("p a b -> p (a b)"), self.selfwd[:],
                            True, True, [uk, "selfwd"], [pk2])
                c0 = 64 * t + 16 * blk
                self.copy(V["U"][:, :, c0:c0 + 16], p2[:, 0:256].rearrange("p (g k) -> p g k", g=16), [pk2], ["U"], eng="act")

    def ssm_pre(self, l, V):
        d = self.din
        self.R.tag = "ssmpre"
        sp = V["ssmp"]
        self.dma(sp[:].rearrange("p a b -> p (a b)"), d["ssmp%d" % l], (), ["ssmp"])
        self.dma(V["h0"][:].rearrange("p a b -> p (a b)"), d["h0_%d" % l], (), ["h0"])
        S = V["S"]
        names = {}

        def s(name):
            if name not in names:
                names[name] = len(names)
                assert len(names) <= 64
            return S[:, names[name], :], "S_" + name

        def tt(o, a, b, op):
            self.tt(o[0], a[0], b[0], op, [a[1], b[1]], [o[1]])

        def ts(o, a, s1, s2, op0, op1=None):
            self.ts(o[0], a[0], s1, s2, op0, op1, [a[1]], [o[1]])

        def cmul(ore, oim, are, aim, bre, bim):
            t1, t2 = s("ct1"), s("ct2")
            tt(t1, are, bre, ALU.mult)
            tt(t2, aim, bim, ALU.mult)
            tt(ore, t1, t2, ALU.subtract)
            t3, t4 = s("ct3"), s("ct4")
            tt(t3, are, bim, ALU.mult)
            tt(t4, aim, bre, ALU.mult)
            tt(oim, t3, t4, ALU.add)

        lre = (sp[:, :, 0], "ssmp")
        lim = (sp[:, :, 1], "ssmp")
        lst = (sp[:, :, 2], "ssmp")
        step = s("step")
        self.act(step[0], lst[0], ACTF.Exp, ["ssmp"], [step[1]])
        zre, zim = s("zre"), s("zim")
        tt(zre, lre, step, ALU.mult)
        tt(zim, lim, step, ALU.mult)
        ts(zre, zre, 1.0 / 64, None, ALU.mult)
        ts(zim, zim, 1.0 / 64, None, ALU.mult)
        are, aim = s("acre0"), s("acim0")
        ts(are, zre, 1.0 / 7, 1.0, ALU.mult, ALU.add)
        ts(aim, zim, 1.0 / 7, None, ALU.mult)
        flip = 0
        for n in (6, 5, 4, 3, 2):
            flip ^= 1
            nre, nim = s("acre%d" % flip), s("acim%d" % flip)
            cmul(nre, nim, zre, zim, are, aim)
            ts(nre, nre, 1.0 / n, 1.0, ALU.mult, ALU.add)
            ts(nim, nim, 1.0 / n, None, ALU.mult)
            are, aim = nre, nim
        wre, wim = s("wre0"), s("wim0")
        cmul(wre, wim, zre, zim, are, aim)
        flip = 0
        for _ in range(6):
            flip ^= 1
            nre, nim = s("wre%d" % flip), s("wim%d" % flip)
            t1, t2, t3 = s("dt1"), s("dt2"), s("dt3")
            tt(t1, wre, wre, ALU.mult)
            tt(t2, wim, wim, ALU.mult)
            tt(t1, t1, t2, ALU.subtract)
            self.stt(nre[0], wre[0], 2.0, t1[0], ALU.mult, ALU.add, [wre[1], t1[1]], [nre[1]])
            ts(t3, wre, 1.0, None, ALU.add)
            tt(t3, t3, wim, ALU.mult)
            ts(nim, t3, 2.0, None, ALU.mult)
            wre, wim = nre, nim
        lbre, lbim = s("lbre"), s("lbim")
        ts(lbre, wre, 1.0, None, ALU.add)
        ts(lbim, wim, 1.0, None, ALU.mult)
        den, t1 = s("den"), s("kt1")
        tt(den, lre, lre, ALU.mult)
        tt(t1, lim, lim, ALU.mult)
        tt(den, den, t1, ALU.add)
        self.recip(den[0], den[0], [den[1]], [den[1]])
        ire, iim = s("ire"), s("iim")
        tt(ire, lre, den, ALU.mult)
        tt(iim, lim, den, ALU.mult)
        ts(iim, iim, -1.0, None, ALU.mult)
        kre, kim = s("kre"), s("kim")
        cmul(kre, kim, wre, wim, ire, iim)
        Bre, Bim = sp[:, :, 3:19], sp[:, :, 19:35]
        bb = V["bbar"]
        T1, T2 = V["tmpA"], V["tmpB"]
        kreb = kre[0].unsqueeze(2).to_broadcast([128, 16, 16])
        kimb = kim[0].unsqueeze(2).to_broadcast([128, 16, 16])
        ta, tb = T1[:].rearrange("p a b c -> p (a b c)")[:, 0:256].rearrange("p (a b) -> p a b", a=16), \
            T2[:].rearrange("p a b c -> p (a b c)")[:, 0:256].rearrange("p (a b) -> p a b", a=16)
        self.tt(ta, Bre, kreb, ALU.mult, ["ssmp", kre[1]], ["tmpA"])
        self.tt(tb, Bim, kimb, ALU.mult, ["ssmp", kim[1]], ["tmpB"])
        self.tt(bb[:, 0, :, :], ta, tb, ALU.subtract, ["tmpA", "tmpB"], ["bbar"])
        self.tt(ta, Bim, kreb, ALU.mult, ["ssmp", kre[1]], ["tmpA"])
        self.tt(tb, Bre, kimb, ALU.mult, ["ssmp", kim[1]], ["tmpB"])
        self.tt(bb[:, 1, :, :], ta, tb, ALU.add, ["tmpA", "tmpB"], ["bbar"])
        L = V["L"]
        pre, pim = lbre, lbim
        for n in range(1, 9):
            if n > 1:
                nre, nim = s("pre%d" % (n % 2)), s("pim%d" % (n % 2))
                cmul(nre, nim, pre, pim, lbre, lbim)
                pre, pim = nre, nim
            for r, src in ((0, pre), (1, pim)):
                self.copy(L[:, r, 0:8, n - 1], src[0][:, 0:8], [src[1]], ["L"])
                self.copy(L[:, r, 8:16, 8 - n], src[0][:, 8:16], [src[1]], ["L"])
        lamp = V["lamp"]
        qre, qim = pre, pim
        for sidx in range(8):
            if sidx > 0:
                nre, nim = s("qre%d" % (sidx % 2)), s("qim%d" % (sidx % 2))
                cmul(nre, nim, qre, qim, qre, qim)
                qre, qim = nre, nim
            self.copy(lamp[:, :, sidx, 0], qre[0], [qre[1]], ["lamp"])
            self.copy(lamp[:, :, sidx, 1], qim[0], [qim[1]], ["lamp"])
            self.ts(lamp[:, :, sidx, 2], qim[0], -1.0, None, ALU.mult, None, [qim[1]], ["lamp"])
        h0 = V["h0"]
        lh0 = V["lh0"]
        lamre, lamim = (lamp[:, :, 0, 0], "lamp"), (lamp[:, :, 0, 1], "lamp")
        cmul((lh0[:, :, 0], "lh0"), (lh0[:, :, 1], "lh0"), lamre, lamim, (h0[:, :, 0], "h0"), (h0[:, :, 1], "h0"))
        Cre, Cim = sp[:, :, 35:51], sp[:, :, 51:67]
        WH = V["WH"]
        for hf in range(2):
            tsl = slice(hf * 8, hf * 8 + 8)
            creb = Cre[:, tsl, :].unsqueeze(2).to_broadcast([128, 8, 8, 16])
            cimb = Cim[:, tsl, :].unsqueeze(2).to_broadcast([128, 8, 8, 16])
            Lreb = L[:, 0, tsl, :].unsqueeze(3).to_broadcast([128, 8, 8, 16])
            Limb = L[:, 1, tsl, :].unsqueeze(3).to_broadcast([128, 8, 8, 16])
            self.tt(T1[:], creb, Lreb, ALU.mult, ["ssmp", "L"], ["tmpA"])
            self.tt(T2[:], cimb, Limb, ALU.mult, ["ssmp", "L"], ["tmpB"])
            self.tt(WH[:, tsl, 0, :].rearrange("p a (j c) -> p a j c", j=8), T1[:], T2[:], ALU.subtract, ["tmpA", "tmpB"], ["WH"])
            self.tt(T1[:], creb, Limb, ALU.mult, ["ssmp", "L"], ["tmpA"])
            self.tt(T2[:], cimb, Lreb, ALU.mult, ["ssmp", "L"], ["tmpB"])
            self.tt(T1[:], T1[:], T2[:], ALU.add, ["tmpA", "tmpB"], ["tmpA"])
            self.ts(WH[:, tsl, 1, :].rearrange("p a (j c) -> p a j c", j=8), T1[:], -1.0, None, ALU.mult, None, ["tmpA"], ["WH"])
        cimn = V["cimn"]
        self.ts(cimn[:], Cim, -1.0, None, ALU.mult, None, ["ssmp"], ["cimn"])
        H = V["H"]
        KM = V["KM"]
        WS = V["WS"]
        for grp in range(4):
            tds = [grp * 4 + i for i in range(4)]
            pk0, pk1 = self.ps[6], self.ps[7]
            for step_i in range(8):
                cur, prv = step_i % 2, (step_i + 1) % 2
                for sl, td in enumerate(tds):
                    dirn = td // 8
                    pos = step_i if dirn == 0 else 7 - step_i
                    hk_c, hk_p = "H%d_%d" % (cur, sl), "H%d_%d" % (prv, sl)
                    Hc_re, Hc_im = H[:, cur, sl, 0, :], H[:, cur, sl, 1, :]
                    Hp_re, Hp_im = H[:, prv, sl, 0, :], H[:, prv, sl, 1, :]
                    lr, li = lbre[0][:, td:td + 1], lbim[0][:, td:td + 1]
                    if step_i == 0:
                        self.memset(H[:, cur, sl, :, :], 0.0, [hk_c])
                    else:
                        t, tk = self.f32()
                        self.ts(t[:, 0:128], Hp_im, li, None, ALU.mult, None, [hk_p, lbim[1]], [tk])
                        self.stt(Hc_re, Hp_re, lr, t[:, 0:128], ALU.mult, ALU.subtract, [hk_p, tk, lbre[1]], [hk_c])
                        t2, tk2 = self.f32()
                        self.ts(t2[:, 0:128], Hp_re, li, None, ALU.mult, None, [hk_p, lbim[1]], [tk2])
                        self.stt(Hc_im, Hp_im, lr, t2[:, 0:128], ALU.mult, ALU.add, [hk_p, tk2, lbre[1]], [hk_c])
                    self.copy(H[:, cur, sl, :, pos * 16:(pos + 1) * 16], bb[:, :, td, :], ["bbar", hk_c], [hk_c])
                    for gl in range(2):
                        reg = sl * 2 + gl
                        pt = pk0 if reg < 4 else pk1
                        pkey = "ps6" if reg < 4 else "ps7"
                        o = pt[:, (reg % 4) * 128 + pos * 16:(reg % 4) * 128 + pos * 16 + 16]
                        rows = slice(gl * 64, gl * 64 + 64)
                        self.mm(o, H[rows, cur, sl, 0, :], sp[rows, td, 35:51], True, False, [hk_c, "ssmp"], [pkey])
                        self.mm(o, H[rows, cur, sl, 1, :], cimn[rows, td, :], False, True, [hk_c, "cimn"], [pkey])
            for sl, td in enumerate(tds):
                dirn, gp = td // 8, td % 8
                for gl in range(2):
                    reg = sl * 2 + gl
                    pt = pk0 if reg < 4 else pk1
                    pkey = "ps6" if reg < 4 else "ps7"
                    self.copy(KM[:, 2 * gp + gl, dirn, :], pt[:, (reg % 4) * 128:(reg % 4) * 128 + 128], [pkey], ["KM"], eng="act")
                for r in range(2):
                    p, pk = self.psum()
                    self.tr(p[:, 0:128], H[:, 1, sl, r, :], self.identf[:], ["H1_%d" % sl, "identf"], [pk])
                    self.copy(WS[:, td, r, :], p[:, 0:128], [pk], ["WS"], eng="act")

    def ssm_main(self, l, V):
        U, WS, WH, KM = V["U"], V["WS"], V["WH"], V["KM"]
        self.R.tag = "scan"
        lamp, lh0, h0 = V["lamp"], V["lh0"], V["h0"]
        Ep = V["Eprev"]
        nss = V["nss"]
        for dirn, half in ((0, 0), (0, 1), (1, 0), (1, 1)):
            E = [V["E0"], V["E1"]]
            cur = 0
            gps = [half * 4 + i for i in range(4)]
            for gi, gp in enumerate(gps):
                td = dirn * 8 + gp
                pr, prk = self.psum()
                pi, pik = self.psum()
                for gl in range(2):
                    rows = slice(gl * 64, gl * 64 + 64)
                    g = 2 * gp + gl
                    self.mm(pr[rows, 0:NCH], WS[:, td, 0, gl * 64:gl * 64 + 64], U[:, g, :], True, True, ["WS", "U"], [prk])
                    self.mm(pi[rows, 0:NCH], WS[:, td, 1, gl * 64:gl * 64 + 64], U[:, g, :], True, True, ["WS", "U"], [pik])
                self.copy(E[0][:, gi, 0, :], pr[:, 0:NCH], [prk], ["E0s", "E0p"], eng="act")
                self.copy(E[0][:, gi, 1, :], pi[:, 0:NCH], [pik], ["E0s", "E0p"], eng="act")
            col = 0 if dirn == 0 else 255
            td0 = dirn * 8 + half * 4
            self.tt(E[0][:, :, :, col], E[0][:, :, :, col], lh0[:, td0:td0 + 4, :], ALU.add, ["E0s", "lh0"], ["E0s"])
            for sidx in range(8):
                sh = 1 << sidx
                src, dst = E[cur], E[1 - cur]
                sk, dk = "E%ds" % cur, "E%ds" % (1 - cur)
                skp, dkp = "E%dp" % cur, "E%dp" % (1 - cur)
                ranges = [(0, 256, 1)]
                if sh < 32:
                    ranges.append((256, 32, 2))
                for gi, gp in enumerate(gps):
                    td = dirn * 8 + gp
                    lr = lamp[:, td, sidx, 0:1]
                    li = lamp[:, td, sidx, 1:2]
                    lin = lamp[:, td, sidx, 2:3]
                    for (base, ln, nseq) in ranges:
                        def v(buf, r, off, n):
                            a = buf[:, gi, r, base:base + ln * nseq]
                            if nseq == 2:
                                return a.rearrange("p (s k) -> p s k", s=2)[:, :, off:off + n]
                            return a[:, off:off + n]
                        n = ln - sh
                        if dirn == 0:
                            o_off, s_off, keep_off = sh, 0, 0
                        else:
                            o_off, s_off, keep_off = 0, sh, ln - sh
                        if nseq == 1:
                            t, tk = self.f32()
                            tv3 = t[:, 0:2 * n].rearrange("p (r k) -> p r k", r=2)
                            self.stt(tv3, src[:, gi, :, base + s_off:base + s_off + n], lr,
                                     src[:, gi, :, base + o_off:base + o_off + n], ALU.mult, ALU.add, [sk, "lamp"], [tk])
                            self.stt(v(dst, 0, o_off, n), v(src, 1, s_off, n), lin, tv3[:, 0, :], ALU.mult, ALU.add, [sk, tk, "lamp"], [dk])
                            self.stt(v(dst, 1, o_off, n), v(src, 0, s_off, n), li, tv3[:, 1, :], ALU.mult, ALU.add, [sk, tk, "lamp"], [dk])
                            self.copy(dst[:, gi, :, base + keep_off:base + keep_off + sh], src[:, gi, :, base + keep_off:base + keep_off + sh],
                                      [sk], [dk])
                            continue
                        pa, pb_ = self.pscr[0][:, :, 0:n], self.pscr[1][:, :, 0:n]

                        def pts(o_, i_, sc, rd, wr):
                            self.R.op("pool", lambda e: e.tensor_scalar(o_, i_, sc, 0.0, ALU.mult, ALU.add), rd, wr)

                        def ptt(o_, i0, i1, rd, wr):
                            self.R.op("pool", lambda e: e.tensor_tensor(o_, i0, i1, ALU.add), rd, wr)
                        pts(pa, v(src, 0, s_off, n), lr, [skp, "lamp"], ["pscr0"])
                        ptt(pa, pa, v(src, 0, o_off, n), [skp, "pscr0"], ["pscr0"])
                        pts(pb_, v(src, 1, s_off, n), lin, [skp, "lamp"], ["pscr1"])
                        ptt(v(dst, 0, o_off, n), pa, pb_, ["pscr0", "pscr1"], [dkp])
                        pts(pa, v(src, 1, s_off, n), lr, [skp, "lamp"], ["pscr0"])
                        ptt(pa, pa, v(src, 1, o_off, n), [skp, "pscr0"], ["pscr0"])
                        pts(pb_, v(src, 0, s_off, n), li, [skp, "lamp"], ["pscr1"])
                        ptt(v(dst, 1, o_off, n), pa, pb_, ["pscr0", "pscr1"], [dkp])
                        for r in range(2):
                            self.copy(v(dst, r, keep_off, sh), v(src, r, keep_off, sh), [skp], [dkp], eng="pool")
                cur = 1 - cur
            Es = E[cur]
            esk = "E%ds" % cur
            Epr = E[1]
            epk = "E1p"
            for gi, gp in enumerate(gps):
                td = dirn * 8 + gp
                for r in range(2):
                    pv = Ep[:, td, r, 256:320].rearrange("p (s k) -> p s k", s=2)
                    sv = Epr[:, gi, r, 256:320].rearrange("p (s k) -> p s k", s=2)
                    if dirn == 0:
                        self.copy(Ep[:, td, r, 1:256], Es[:, gi, r, 0:255], [esk], ["Eprev"], eng="act")
                        self.copy(Ep[:, td, r, 0:1], h0[:, td, r:r + 1], ["h0"], ["Eprev"])
                        self.copy(pv[:, :, 1:32], sv[:, :, 0:31], [epk], ["Eprev"], eng="act")
                        self.memset(pv[:, :, 0:1], 0.0, ["Eprev"])
                        self.copy(nss[:, td, :, r], sv[:, :, 31], [epk], ["nss"])
                    else:
                        self.copy(Ep[:, td, r, 0:255], Es[:, gi, r, 1:256], [esk], ["Eprev"], eng="act")
                        self.copy(Ep[:, td, r, 255:256], h0[:, td, r:r + 1], ["h0"], ["Eprev"])
                        self.copy(pv[:, :, 0:31], sv[:, :, 1:32], [epk], ["Eprev"], eng="act")
                        self.memset(pv[:, :, 31:32], 0.0, ["Eprev"])
                        self.copy(nss[:, td, :, r], sv[:, :, 0], [epk], ["nss"])
        self.R.barrier()
        self.R.tag = "ssmout"
        self.dma(self.dout["nssm"][l], nss[:].rearrange("p a b c -> p (a b c)"), ["nss"], ["o_nssm%d" % l])
        Ysb = V["Ysb"]
        for g in range(16):
            gp, gl = g // 2, g % 2
            rows = slice(gl * 64, gl * 64 + 64)
            p, pk = self.psum()
            first = True
            for dirn in range(2):
                td = dirn * 8 + gp
                self.mm(p[:, 0:NCH], KM[:, g, dirn, :], U[:, g, :], first, False, ["KM", "U"], [pk])
                first = False
                self.mm(p[:, 0:NCH], WH[rows, td, 0, :], Ep[rows, td, 0, :], False, False, ["WH", "Eprev"], [pk])
                self.mm(p[:, 0:NCH], WH[rows, td, 1, :], Ep[rows, td, 1, :], False, dirn == 1, ["WH", "Eprev"], [pk])
            self.copy(Ysb[:, g, :], p[:, 0:NCH], [pk], ["Ysb"], eng="act")
        for t in range(NT):
            c0 = 64 * t
            yg = []
            for hh in range(2):
                p, pk = self.psum()
                for g8 in range(8):
                    g = hh * 8 + g8
                    rep, rk = self.b16()
                    in0 = Ysb[:, g, c0:c0 + 64].unsqueeze(2).to_broadcast([128, 64, 8])
                    in1 = self.jmask[:].unsqueeze(1).to_broadcast([128, 64, 8])
                    self.tt(rep[:].rearrange("p (k j) -> p k j", j=8), in0, in1, ALU.mult, ["Ysb", "jmask"], [rk])
                    self.mm(p[:], self.revsel[:, g8, :], rep[:], g8 == 0, g8 == 7, ["revsel", rk], [pk])
                yv, yk = self.f32()
                self.tt(yv[:], p[:], V["du"][:, hh, t * TT:(t + 1) * TT], ALU.add, [pk, "du"], [yk])
                y2, y2k = self.f32()
                self.act(y2[:], yv[:], ACTF.Square, [yk], [y2k])
                self.ts(y2[:], y2[:], 0.044715, 1.0, ALU.mult, ALU.add, [y2k], [y2k])
                self.tt(y2[:], y2[:], yv[:], ALU.mult, [y2k, yk], [y2k])
                self.act(y2[:], y2[:], ACTF.Tanh, [y2k], [y2k], scale=GELU_C)
                ygf, ygb = V["ygf"], V["ygb"]
                self.stt(ygf[:, hh, :], y2[:], 1.0, yv[:], ALU.add, ALU.mult, [y2k, yk], ["ygf%d" % hh])
                self.copy(ygb[:, hh, :], ygf[:, hh, :], ["ygf%d" % hh], ["ygb%d" % hh], eng="act")
                yg.append((ygf[:, hh, :], "ygf%d" % hh, ygb[:, hh, :], "ygb%d" % hh))
            for oc in range(2):
                p, pk = self.psum()
                for kc in range(2):
                    self.mm(p[:], self.wglu[:, kc, oc * 128:(oc + 1) * 128], yg[kc][2], kc == 0, kc == 1,
                            ["wglu", yg[kc][3]], [pk])
                e, ek = self.f32()
                self.act(e[:], p[:], ACTF.Tanh, [pk], [ek], scale=0.25)
                self.stt(e[:], e[:], 1.0, yg[oc][0], ALU.add, ALU.mult, [ek, yg[oc][1]], [ek])
                self.ts(self.obT[:, oc, t * TT:(t + 1) * TT], e[:], 0.25, None, ALU.mult, None, [ek], ["obT"])

    def build_masks(self, l, V):
        nab = self.din["nab%d" % l]
        self.R.tag = "masks"
        for pi in range(5):
            self.dma(self.mq[pi], self.din["c_negfill"], (), ["mqf%d" % pi])
        W0 = {0: (0, 0), 1: (0, 0), 2: (0, 1), 3: (2, 2), 4: (2, 2)}
        DR0 = {0: (7, 6), 1: (5, 4), 2: (3, 3), 3: (3, 2), 4: (1, 0)}
        wkeys = {pi: [] for pi in range(5)}
        for pi in range(5):
            for i in range(2):
                w0, dr0 = W0[pi][i], DR0[pi][i]
                regs = [(0, 8, 0, 15, -1, 0), (8, 49, 0, 7, 0, 1), (57, 7, 48, 6, -1, 0)]
                for ri, (q0, nq, cs0, dc0, sq, dq) in enumerate(regs):
                    for h in range(4):
                        src = bass.AP(nab.tensor, h * 465 + dr0 * 31 + dc0, [[31, 8], [sq, nq], [1, 16]])
                        dst = bass.AP(self.mq.tensor, pi * 128 * 2560 + (i * 64 + q0) * 2560 + h * 640 + w0 * 64 + cs0,
                                      [[64, 8], [2560 + dq, nq], [1, 16]])
                        wk = "mqw%d_%d_%d_%d" % (pi, i, ri, h)
                        wkeys[pi].append(wk)
                        self.dma(dst, src, ["mqf%d" % pi], [wk])
        for pi in range(5):
            mqs = V["mqs"][pi % 2]
            mts = V["mts"][pi % 2]
            mqk, mtk = "mqs%d" % (pi % 2), "mts%d" % (pi % 2)
            self.dma(mqs[:].rearrange("p a b -> p (a b)"), self.mq[pi], wkeys[pi] + ["mqf%d" % pi], [mqk])
            for h in range(4):
                for half in range(2):
                    p, pk = self.psum()
                    njj = 4 if half == 0 else 1
                    for jj in range(njj):
                        j = half * 4 + jj
                        self.tr(p[:, jj * 128:(jj + 1) * 128], mqs[:, h, j * 128:(j + 1) * 128], self.identf[:],
                                [mqk, "identf"], [pk])
                    self.copy(mts[:, h, half * 4:half * 4 + njj, :].rearrange("p a b -> p (a b)"), p[:, 0:njj * 128], [pk], [mtk],
                              eng="act")
            self.dma(self.mt[pi], mts[:].rearrange("p a b c -> p (a b c)"), [mtk], ["mt%d" % pi])

    def kv_bound_update(self, src_ap, skey, col, n, first):
        sq, sk = self.b16()
        self.act(sq[0:64, 0:n], src_ap, ACTF.Square, [skey], [sk])
        p, pk = self.psum()
        self.mm(p[0:65, 0:n], self.eaug[:], sq[0:64, 0:n], True, True, [sk, "eaug"], [pk])
        if first:
            self.R.op("dve", lambda e: e.reduce_max(self.kmax[64:65, col:col + 1], p[64:65, 0:n], AX.X), [pk], ["kmax"])
        else:
            self.R.op("dve", lambda e: e.reduce_max(self.ktmp[64:65, col:col + 1], p[64:65, 0:n], AX.X), [pk], ["ktmp"])
            self.tt(self.kmax[64:65, col:col + 1], self.kmax[64:65, col:col + 1], self.ktmp[64:65, col:col + 1], ALU.max,
                    ["kmax", "ktmp"], ["kmax"])

    def head_norm(self, p, pk, gcol):
        sq, sk = self.b16()
        self.act(sq[0:64, :], p[0:64, :], ACTF.Square, [pk], [sk])
        p2, pk2 = self.psum()
        self.mm(p2[0:64, :], self.onesb[0:64, 0:64], sq[0:64, :], True, True, [sk, "onesb"], [pk2])
        r, rk = self.f32()
        self.act(r[0:64, :], p2[0:64, :], ACTF.Ln, [pk2], [rk], bias=self.epsb[0:64, :], scale=1.0 / 64)
        self.act(r[0:64, :], r[0:64, :], ACTF.Exp, [rk], [rk], scale=-0.5)
        o, ok = self.f32()
        self.stt(o[0:64, :], p[0:64, :], self.small[0:64, gcol:gcol + 1], r[0:64, :], ALU.mult, ALU.mult, [pk, rk, "small"], [ok])
        return o, ok

    def rope(self, x, xk, V, t):
        p, pk = self.psum()
        self.mm(p[0:64, :], self.prot[:], x[0:64, :], True, True, [xk, "prot"], [pk])
        t1, k1 = self.f32()
        self.tt(t1[0:64, :], x[0:64, :], V["rope"][:, 0, :], ALU.mult, [xk, "rope"], [k1])
        t2, k2 = self.f32()
        self.tt(t2[0:64, :], p[0:64, :], V["rope"][:, 1, :], ALU.mult, [pk, "rope"], [k2])
        self.tt(t1[0:64, :], t1[0:64, :], t2[0:64, :], ALU.add, [k1, k2], [k1])
        return t1, k1

    def load_rope(self, V, t):
        self.dma(V["rope"][:, 0, :], self.din["c_ropec"][:, t * TT:(t + 1) * TT], (), ["rope"])
        self.dma(V["rope"][:, 1, :], self.din["c_ropes"][:, t * TT:(t + 1) * TT], (), ["rope"])

    def init_stores(self, l, V, sample):
        d = self.din
        KaT, Va, KcT, Vc = V["KaT"], V["Va"], V["KcT"], V["Vc"]
        self.memset(KaT[64:128, :, :], 0.0, ["KaT"])
        self.memset(KcT[64:128, :, :], 0.0, ["KcT"])
        self.memset(KaT[64:65, :, :], 1.0, ["KaT"])
        self.memset(KcT[64:65, :, :], 1.0, ["KcT"])
        self.memset(V["qaug"][0][64:128, :], 0.0, ["qaug0"])
        self.memset(V["qaug"][1][64:128, :], 0.0, ["qaug1"])
        self.memset(Va[:].rearrange("p a b c -> p (a b c)"), 0.0, ["Va"])
        self.memset(Vc[:].rearrange("p a b c -> p (a b c)"), 0.0, ["Vc"], eng="pool")
        self.memset(Va[:, :, :, 64:65], 1.0, ["Va"])
        self.memset(Vc[:, :, :, 64:65], 1.0, ["Vc"])
        if sample:
            for g in range(2):
                self.dma(KaT[0:64, g, 0:256], d["ckaT%d" % l][:, g * 256:(g + 1) * 256], (), ["KaT"], eng="pool", chan="cst")
            for h in range(4):
                self.dma(KcT[0:64, h, 0:256], d["ckcT%d" % l][:, h * 256:(h + 1) * 256], (), ["KcT"], eng="pool", chan="cst")
            for a_ in range(2):
                self.dma(Va[:, a_, 0:2, 0:64], d["cva%d" % l][:, a_ * 128:(a_ + 1) * 128].rearrange("p (g e) -> p g e", g=2), (), ["Va"], eng="pool", chan="cst")
                self.dma(Vc[:, a_, 0:4, 0:64], d["cvc%d" % l][:, a_ * 256:(a_ + 1) * 256].rearrange("p (g e) -> p g e", g=4), (), ["Vc"], eng="pool", chan="cst")
            for g in range(2):
                self.kv_bound_update(KaT[0:64, g, 0:256], "KaT", g, 256, True)
            for h in range(4):
                self.kv_bound_update(KcT[0:64, h, 0:256], "KcT", 2 + h, 256, True)

    def phase_a2(self, l, t, V):
        d = self.din
        self.R.tag = "a2"
        sample = t < 4
        cv = 0 if sample else 1
        KaT, Va, KcT, Vc = V["KaT"], V["Va"], V["KcT"], V["Vc"]
        self.load_x(l, t)
        self.norm_mod(self.xt, "xt", self.a1, 0, cv, self.hT, "hT")
        w, wk = self.wload(d["wkv%d" % l][0], 4096)
        w2, wk2 = self.wload(d["wkc%d" % l][0], 2048)
        if self.sub(1): return
        if sample:
            self.load_rope(V, t)
            key0 = 256 + t * TT
        else:
            key0 = 0
        if self.sub(2): return
        for g in range(2):
            p, pk = self.psum()
            for k in range(KC):
                self.mm(p[0:64, :], w[:, k * 512 + g * 64:k * 512 + g * 64 + 64], self.hT[:, k, :], k == 0, k == KC - 1,
                        [wk, "hT"], [pk])
            if self.sub(3): return
            kn, knk = self.head_norm(p, pk, 75)
            if self.sub(4): return
            if sample:
                kn, knk = self.rope(kn, knk, V, t)
            else:
                self.dma(self.dout["nk_a"][l][:, g * 512:(g + 1) * 512], kn[0:64, :], [knk], ["o_nka%d_%d" % (l, g)])
            if self.sub(5): return
            self.copy(KaT[0:64, g, key0:key0 + TT], kn[0:64, :], [knk], ["KaT"], eng="act")
            if self.sub(6): return
            first = (not sample)
            self.kv_bound_update(kn[0:64, :], knk, g, TT, first)
        if self.sub(7): return
        for h in range(4):
            p, pk = self.psum()
            for k in range(KC):
                self.mm(p[0:64, :], w2[:, k * 256 + h * 64:k * 256 + h * 64 + 64], self.hT[:, k, :], k == 0, k == KC - 1,
                        [wk2, "hT"], [pk])
            kf, kfk = self.f32()
            self.copy(kf[0:64, :], p[0:64, :], [pk], [kfk], eng="act")
            if not sample:
                self.dma(self.dout["nk_c"][l][:, h * 512:(h + 1) * 512], kf[0:64, :], [kfk], ["o_nkc%d_%d" % (l, h)])
            self.copy(KcT[0:64, h, key0:key0 + TT], kf[0:64, :], [kfk], ["KcT"])
            self.kv_bound_update(kf[0:64, :], kfk, 2 + h, TT, not sample)
        if self.sub(8): return
        for blk in range(4):
            p, pk = self.psum()
            for k in range(KC):
                self.mm(p[:, 0:384], self.hT[:, k, blk * 128:(blk + 1) * 128], w[:, k * 512 + 128:k * 512 + 512],
                        k == 0, k == KC - 1, [wk, "hT"], [pk])
            ch = (2 + 4 * t + blk) if sample else blk
            if self.sub(9): return
            self.copy(Va[:, ch, 0:2, 0:64], p[:, 0:128].rearrange("p (g e) -> p g e", g=2), [pk], ["Va"], eng="act")
            if self.sub(10): return
            self.copy(Vc[:, ch, 0:4, 0:64], p[:, 128:384].rearrange("p (g e) -> p g e", g=4), [pk], ["Vc"])
            if not sample:
                vf, vfk = self.f32()
                self.copy(vf[:, 0:384], p[:, 0:384], [pk], [vfk])
                self.dma(self.dout["nv_a"][l][:, blk * 128:(blk + 1) * 128], vf[:, 0:128], [vfk], ["o_nva%d_%d" % (l, blk)])
                self.dma(self.dout["nv_c"][l][:, blk * 256:(blk + 1) * 256], vf[:, 128:384], [vfk], ["o_nvc%d_%d" % (l, blk)])

    def q_bound(self, qsrc, qk, qa, qak, col, n):
        sq, sk = self.b16()
        self.act(sq[0:64, 0:n], qsrc, ACTF.Square, [qk], [sk])
        p, pk = self.psum()
        self.mm(p[0:65, 0:n], self.eaug[:], sq[0:64, 0:n], True, True, [sk, "eaug"], [pk])
        r, rk = self.f32()
        self.act(r[64:65, 0:n], p[64:65, 0:n], ACTF.Ln, [pk, "kmax"], [rk], scale=self.kmax[64:65, col:col + 1],
                 bias=self.tinyb[64:65, :])
        self.act(r[64:65, 0:n], r[64:65, 0:n], ACTF.Exp, [rk], [rk], scale=0.5)
        self.ts(qa[64:65, 0:n], r[64:65, 0:n], -0.125, None, ALU.mult, None, [rk], [qak])

    def attn_core(self, qa, qak, n, kch, vch, dst, dkey, po, pok, masks=None, ocol=0):
        nchk = len(kch)
        for j in range(nchk):
            ps_, psk = self.psum()
            has_mask = masks is not None and masks[j] is not None
            self.mm(ps_[:, 0:n], kch[j][0], qa[0:65, 0:n], True, not has_mask, [kch[j][1], qak], [psk])
            if has_mask:
                self.mm(ps_[:, 0:n], self.identb[:], masks[j][0], False, True, ["identb", masks[j][1]], [psk])
            pt, ptk = self.b16()
            self.act(pt[:, 0:n], ps_[:, 0:n], ACTF.Exp, [psk], [ptk])
            self.mm(po[0:65, ocol:ocol + n], vch[j][0], pt[:, 0:n], j == 0, j == nchk - 1, [vch[j][1], ptk], [pok])

    def attn_stream(self, jobs, look=2, bg=None, tag=""):
        groups = []
        for job in jobs:
            n = job["n"]
            per = 512 // n
            nch = len(job["kch"])
            for c0 in range(0, nch, per):
                groups.append((job, list(range(c0, min(c0 + per, nch)))))
        pend = []

        def emit_pv(item):
            job, chunks, pt, ptk = item
            n = job["n"]
            nch = len(job["kch"])
            for i, c in enumerate(chunks):
                self.mm(job["po"][0:128, job["ocol"]:job["ocol"] + n], job["vch"][c][0], pt[:, i * n:(i + 1) * n],
                        c == 0, c == nch - 1, [job["vch"][c][1], ptk], [job["pok"]])
        for gi_, (job, chunks) in enumerate(groups):
            if bg and ((gi_ >= 2 and gi_ % 2 == 0) or len(groups) < 12):
                for g_ in list(bg):
                    try:
                        next(g_)
                    except StopIteration:
                        bg.remove(g_)
            self.R.tag = tag
            n = job["n"]
            ps_, psk = self.psum()
            for i, c in enumerate(chunks):
                m = job["masks"][c] if job["masks"] else None
                self.mm(ps_[:, i * n:(i + 1) * n], job["kch"][c][0], job["qa"], True, m is None,
                        [job["kch"][c][1], job["qak"]], [psk])
                if m is not None:
                    self.mm(ps_[:, i * n:(i + 1) * n], self.identb[:], m[0], False, True, ["identb", m[1]], [psk])
            pt, ptk = self.b16()
            w = len(chunks) * n
            self.act(pt[:, 0:w], ps_[:, 0:w], ACTF.Exp, [psk], [ptk])
            pend.append((job, chunks, pt, ptk))
            if len(pend) > look:
                emit_pv(pend.pop(0))
        while pend:
            emit_pv(pend.pop(0))
        if bg:
            for g_ in bg:
                for _ in g_:
                    pass

    def attn_finish_gen(self, po, pok, n, dst, dkey, tag=""):
        yield
        yield
        self.R.tag = tag
        self.act(self.rsc[64:65, 0:n], po[64:65, 0:n], ACTF.Ln, [pok], ["rsc"])
        self.act(self.rsc[64:65, 0:n], self.rsc[64:65, 0:n], ACTF.Exp, ["rsc"], ["rsc"], scale=-1.0)
        o = self.osb
        self.copy(o[0:64, 0:n], po[0:64, 0:n], [pok], ["osb"])
        yield
        self.R.tag = tag
        pb, pbk = self.psum()
        self.mm(pb[0:64, 0:n], self.esel[:], self.rsc[0:65, 0:n], True, True, ["rsc", "esel"], [pbk])
        yield
        self.R.tag = tag
        self.tt(dst, o[0:64, 0:n], pb[0:64, 0:n], ALU.mult, ["osb", pbk], [dkey])

    def phase_c(self, l, t, V, last):
        d = self.din
        self.R.tag = "c_norm"
        sample = t < 4
        cv = 0 if sample else 1
        KaT, Va, KcT, Vc = V["KaT"], V["Va"], V["KcT"], V["Vc"]
        oaT, ocT = V["oaT"], V["ocT"]
        self.load_x(l, t)
        self.norm_mod(self.xt, "xt", self.a1, 0, cv, self.hT, "hT")
        if sample:
            self.load_rope(V, t)
        self.R.tag = "c_attnA"
        wqa, wqak = self.wload(d["wqa%d" % l][0], 4096)
        wqc, wqck = self.wload(d["wqc%d" % l][0], 2048)
        acc = [0]

        P4, P5 = self.ps[4], self.ps[5]

        def prep_gen(kind, h):
            self.R.tag = "c_attn%s.prep" % kind.upper()
            qa = V["qaug"][h % 2]
            qak = "qaug%d" % (h % 2)
            if kind == "a":
                g = h // 4
                for k in range(KC):
                    self.mm(P4[0:64, :], wqa[:, k * 512 + h * 64:k * 512 + h * 64 + 64], self.hT[:, k, :], k == 0, k == KC - 1,
                            [wqak, "hT"], ["ps4"])
                yield
                sq, sk = self.b16()
                pc4, pc4k = self.f32()
                self.copy(pc4[0:64, :], P4[0:64, :], ["ps4"], [pc4k])
                self.tt(sq[0:64, :], pc4[0:64, :], pc4[0:64, :], ALU.mult, [pc4k], [sk])
                self.mm(P5[0:64, :], self.onesb[0:64, 0:64], sq[0:64, :], True, True, [sk, "onesb"], ["ps5"])
                yield
                r, rk = self.f32()
                self.act(r[0:64, :], P5[0:64, :], ACTF.Ln, ["ps5"], [rk], bias=self.epsb[0:64, :], scale=1.0 / 64)
                self.act(r[0:64, :], r[0:64, :], ACTF.Exp, [rk], [rk], scale=-0.5)
                qn, qnk = self.f32()
                self.stt(qn[0:64, :], P4[0:64, :], self.small[0:64, 74:75], r[0:64, :], ALU.mult, ALU.mult, ["ps4", rk, "small"], [qnk])
                yield
                if sample:
                    self.mm(P5[0:64, :], self.prot[:], qn[0:64, :], True, True, [qnk, "prot"], ["ps5"])
                    t1, k1 = self.f32()
                    self.tt(t1[0:64, :], qn[0:64, :], V["rope"][:, 0, :], ALU.mult, [qnk, "rope"], [k1])
                    yield
                    t2, k2 = self.f32()
                    self.tt(t2[0:64, :], P5[0:64, :], V["rope"][:, 1, :], ALU.mult, ["ps5", "rope"], [k2])
                    self.tt(t1[0:64, :], t1[0:64, :], t2[0:64, :], ALU.add, [k1, k2], [k1])
                    qn, qnk = t1, k1
                    yield
                col = g
                qsrc, qsk = qn[0:64, :], qnk
            else:
                for k in range(KC):
                    self.mm(P4[0:64, :], wqc[:, k * 256 + h * 64:k * 256 + h * 64 + 64], self.hT[:, k, :], k == 0, k == KC - 1,
                            [wqck, "hT"], ["ps4"])
                yield
                col = 2 + h
                qsrc, qsk = P4[0:64, :], "ps4"
            self.ts(qa[0:64, :], qsrc, 0.125, None, ALU.mult, None, [qsk], [qak])
            sq, sk = self.b16()
            if kind == "a":
                self.tt(sq[0:64, :], qsrc, qsrc, ALU.mult, [qsk], [sk])
            else:
                self.act(sq[0:64, :], qsrc, ACTF.Square, [qsk], [sk])
            self.mm(P5[0:65, :], self.eaug[:], sq[0:64, :], True, True, [sk, "eaug"], ["ps5"])
            yield
            r, rk = self.f32()
            self.act(r[64:65, :], P5[64:65, :], ACTF.Ln, ["ps5", "kmax"], [rk], scale=self.kmax[64:65, col:col + 1],
                     bias=self.tinyb[64:65, :])
            self.act(r[64:65, :], r[64:65, :], ACTF.Exp, [rk], [rk], scale=0.5)
            self.ts(qa[64:65, :], r[64:65, :], -0.125, None, ALU.mult, None, [rk], [qak])

        def jobs_a(h, qa, qak, po, pok):
            g = h // 4
            if sample:
                return [dict(qa=qa[0:128, :], qak=qak, n=TT, po=po, pok=pok, ocol=0, masks=None,
                             kch=[(KaT[0:128, g, j * 128:(j + 1) * 128], "KaT") for j in range(18)],
                             vch=[(Va[:, j, :, :].rearrange("p a b -> p (a b)")[:, 66 * g:66 * g + 128], "Va") for j in range(18)])]
            return [dict(qa=qa[0:128, s_ * 256:(s_ + 1) * 256], qak=qak, n=256, po=po, pok=pok, ocol=s_ * 256, masks=None,
                         kch=[(KaT[0:128, g, s_ * 256 + j * 128:s_ * 256 + (j + 1) * 128], "KaT") for j in range(2)],
                         vch=[(Va[:, 2 * s_ + j, :, :].rearrange("p a b -> p (a b)")[:, 66 * g:66 * g + 128], "Va") for j in range(2)]) for s_ in range(2)]

        def jobs_c(h, qa, qak, po, pok):
            if not sample:
                return [dict(qa=qa[0:128, s_ * 256:(s_ + 1) * 256], qak=qak, n=256, po=po, pok=pok, ocol=s_ * 256, masks=None,
                             kch=[(KcT[0:128, h, s_ * 256 + j * 128:s_ * 256 + (j + 1) * 128], "KcT") for j in range(2)],
                             vch=[(Vc[:, 2 * s_ + j, :, :].rearrange("p a b -> p (a b)")[:, 66 * h:66 * h + 128], "Vc") for j in range(2)]) for s_ in range(2)]
            jobs = []
            for b4 in range(4):
                b = 4 * t + b4
                pi = {0: 0, 1: 1, 14: 3, 15: 4}.get(b, 2)
                R0 = min(max(2 * b - 4, 0), 22)
                if pi == 2:
                    mb, mbk = V["mgen"], "mgen"
                else:
                    mb, mbk = V["mbrd"][pi % 2], "mbrd%d" % (pi % 2)
                kch = [(KcT[0:128, h, j * 128:(j + 1) * 128], "KcT") for j in range(2)]
                vch = [(Vc[:, j, :, :].rearrange("p a b -> p (a b)")[:, 66 * h:66 * h + 128], "Vc") for j in range(2)]
                masks = [None, None]
                for jj in range(5):
                    kc0 = 256 + 64 * R0 + 128 * jj
                    kch.append((KcT[0:128, h, kc0:kc0 + 128], "KcT"))
                    vch.append((Vc[:, 2 + R0 // 2 + jj, :, :].rearrange("p a b -> p (a b)")[:, 66 * h:66 * h + 128], "Vc"))
                    masks.append((mb[:, h, jj, :], mbk))
                jobs.append(dict(qa=qa[0:128, b4 * 128:(b4 + 1) * 128], qak=qak, n=128, po=po, pok=pok, ocol=b4 * 128,
                                 masks=masks, kch=kch, vch=vch))
            return jobs

        if sample and t in (0, 3):
            for pi in ((0, 1) if t == 0 else (3, 4)):
                self.dma(V["mbrd"][pi % 2][:].rearrange("p a b c -> p (a b c)"), self.mt[pi], ["mt%d" % pi],
                         ["mbrd%d" % (pi % 2)], eng="pool", chan="msk")
        heads = [("a", h) for h in range(8)] + [("c", h) for h in range(4)]
        self.psum_n = 4
        for _ in prep_gen(*heads[0]):
            pass
        fin = None
        for i, (kind, h) in enumerate(heads):
            qa = V["qaug"][h % 2]
            qak = "qaug%d" % (h % 2)
            bgs = []
            if fin is not None:
                bgs.append(fin)
            if i + 1 < len(heads):
                bgs.append(prep_gen(*heads[i + 1]))
            po, pok = self.ps[6 + acc[0] % 2], "ps%d" % (6 + acc[0] % 2)
            acc[0] += 1
            ctag = "c_attn%s.core" % kind.upper()
            jobs = jobs_a(h, qa, qak, po, pok) if kind == "a" else jobs_c(h, qa, qak, po, pok)
            self.attn_stream(jobs, bg=bgs, tag=ctag)
            dst = oaT[:, h, :] if kind == "a" else ocT[:, h, :]
            fin = self.attn_finish_gen(po, pok, TT, dst, "oaT" if kind == "a" else "ocT", ctag)
        for _ in fin:
            pass
        self.psum_n = 6
        self.R.tag = "c_merge"
        mg = V["merged"]
        tok = slice(t * TT, (t + 1) * TT)
        for m in range(8):
            wb, wbk = self.wload(d["wbr%d" % l][m], 1792)
            wg, wgk = self.wload(d["wg%d" % l][m], 8 * 384)
            pa, pak = self.psum()
            for h in range(8):
                self.mm(pa[:], wb[0:64, h * 128:(h + 1) * 128], oaT[:, h, :], h == 0, h == 7, [wbk, "oaT"], [pak])
            pb, pbk = self.psum()
            for k in range(2):
                self.mm(pb[:], wb[:, 1536 + k * 128:1536 + (k + 1) * 128], self.obT[:, k, tok], k == 0, k == 1, [wbk, "obT"], [pbk])
            pc, pck = self.psum()
            for h in range(4):
                self.mm(pc[:], wb[0:64, 1024 + h * 128:1024 + (h + 1) * 128], ocT[:, h, :], h == 0, h == 3, [wbk, "ocT"], [pck])
            accum = None
            for bi, (pp, ppk) in enumerate(((pa, pak), (pb, pbk), (pc, pck))):
                pg, pgk = self.psum()
                for k in range(KC):
                    self.mm(pg[:], wg[:, k * 384 + bi * 128:k * 384 + (bi + 1) * 128], self.hT[:, k, :], k == 0, k == KC - 1,
                            [wgk, "hT"], [pgk])
                e, ek = self.f32()
                self.act(e[:], pg[:], ACTF.Tanh, [pgk], [ek], scale=0.5)
                if bi == 0:
                    self.stt(e[:], e[:], 1.0, pp[:], ALU.add, ALU.mult, [ek, ppk], [ek])
                    accum = (e, ek)
                elif bi == 1:
                    self.stt(e[:], e[:], 1.0, pp[:], ALU.add, ALU.mult, [ek, ppk], [ek])
                    self.tt(accum[0][:], accum[0][:], e[:], ALU.add, [accum[1], ek], [accum[1]])
                else:
                    self.stt(e[:], e[:], 1.0, pp[:], ALU.add, ALU.mult, [ek, ppk], [ek])
                    self.tt(mg[:, m, :], accum[0][:], e[:], ALU.add, [accum[1], ek], ["merged"])
        self.R.tag = "c_out"
        for half in range(2):
            wo, wok = self.wload(d["wout%d" % l][half], 4096)
            for mm_ in range(4):
                m = half * 4 + mm_
                p, pk = self.psum()
                for k in range(KC):
                    self.mm(p[:], wo[:, k * 512 + mm_ * 128:k * 512 + (mm_ + 1) * 128], mg[:, k, :], k == 0, k == KC - 1,
                            [wok, "merged"], [pk])
                self.stt(self.xt[:, m, :], p[:], self.gth[:, 0, m, cv:cv + 1], self.xt[:, m, :], ALU.mult, ALU.add, [pk, "gth", "xt%d" % m], ["xt%d" % m])
        self.R.tag = "c_ffn"
        self.norm_mod(self.xt, "xt", self.a2, 3, cv, self.hT, "hT")
        actT = V["actT"]
        for u in range(11):
            w, wk = self.wload(d["wgu%d" % l][u], 4096)
            for ff in range(2):
                f = 2 * u + ff
                pg, pgk = self.psum()
                for k in range(KC):
                    self.mm(pg[:], w[:, k * 512 + ff * 128:k * 512 + (ff + 1) * 128], self.hT[:, k, :], k == 0, k == KC - 1,
                            [wk, "hT"], [pgk])
                pu, puk = self.psum()
                for k in range(KC):
                    self.mm(pu[:], w[:, k * 512 + 256 + ff * 128:k * 512 + 256 + (ff + 1) * 128], self.hT[:, k, :], k == 0, k == KC - 1,
                            [wk, "hT"], [puk])
                e, ek = self.f32()
                self.act(e[:], pg[:], ACTF.Tanh, [pgk], [ek], scale=0.5)
                self.stt(e[:], e[:], 1.0, pg[:], ALU.add, ALU.mult, [ek, pgk], [ek])
                self.tt(actT[:, f, :], e[:], pu[:], ALU.mult, [ek, puk], ["actT"])
        for m in range(8):
            w, wk = self.wload(d["wd%d" % l][m], NF * 128)
            p, pk = self.psum()
            for f in range(NF):
                self.mm(p[:], w[:, f * 128:(f + 1) * 128], actT[:, f, :], f == 0, f == NF - 1, [wk, "actT"], [pk])
            self.stt(self.xt[:, m, :], p[:], self.gth[:, 1, m, cv:cv + 1], self.xt[:, m, :], ALU.mult, ALU.add, [pk, "gth", "xt%d" % m], ["xt%d" % m])
            if not last:
                self.dma(self.xs[t][:, m * TT:(m + 1) * TT], self.xt[:, m, :], ["xt%d" % m], ["xs%d_%d" % (t, m)])
        if last:
            r, rk = self.rstd_of(self.xt, "xt")
            for k in range(KC):
                self.stt(self.xt[:, k, :], self.xt[:, k, :], self.small[:, 64 + k:65 + k], r[:], ALU.mult, ALU.mult,
                         ["xt%d" % k, rk, "small"], ["xt%d" % k])
                self.dma(self.dout["y_out"][t][:, k * TT:(k + 1) * TT], self.xt[:, k, :], ["xt%d" % k], ["o_y%d_%d" % (t, k)])

    def build(self):
        nc = self.nc
        self.epsb = self.sb("epsb", [128, 1], F32)
        self.tinyb = self.sb("tinyb", [128, 1], F32)
        self.memset(self.epsb[:], EPS, ["epsb"])
        self.memset(self.tinyb[:], 1e-30, ["tinyb"])
        self.setup()
        VB = {}
        o = [0]

        def take(Vd, name, shape, dt):
            esz = 4 if dt == F32 else 2
            n = 1
            for s_ in shape[1:]:
                n *= s_
            Vd[name] = self.uview(o[0], shape, dt)
            o[0] += (n * esz + 31) // 32 * 32
        take(VB, "U", [128, 16, NCH], BF16)
        take(VB, "du", [128, 2, NTOK], F32)
        take(VB, "WS", [128, 16, 2, 128], BF16)
        take(VB, "WH", [128, 16, 2, 128], BF16)
        take(VB, "KM", [128, 16, 2, 128], BF16)
        take(VB, "lamp", [128, 16, 8, 3], F32)
        take(VB, "lh0", [128, 16, 2], F32)
        take(VB, "h0", [128, 16, 2], F32)
        take(VB, "nss", [128, 16, 2, 2], F32)
        base = o[0]
        take(VB, "ssmp", [128, 16, 67], F32)
        take(VB, "S", [128, 64, 16], F32)
        take(VB, "bbar", [128, 2, 16, 16], F32)
        take(VB, "tmpA", [128, 8, 8, 16], F32)
        take(VB, "tmpB", [128, 8, 8, 16], F32)
        take(VB, "L", [128, 2, 16, 8], F32)
        take(VB, "cimn", [128, 16, 16], F32)
        take(VB, "H", [128, 2, 4, 2, 128], F32)
        take(VB, "urep0", [128, 16, 8, 16], BF16)
        take(VB, "urep1", [128, 16, 8, 16], BF16)
        VB["urep"] = [VB["urep0"], VB["urep1"]]
        end1 = o[0]
        o[0] = base
        take(VB, "Eprev", [128, 16, 2, NCH], BF16)
        zone2 = o[0]
        take(VB, "E0", [128, 4, 2, NCH], F32)
        take(VB, "E1", [128, 4, 2, NCH], F32)
        end2 = o[0]
        o[0] = zone2
        take(VB, "Ysb", [128, 16, NCH], F32)
        take(VB, "ygf", [128, 2, 512], F32)
        take(VB, "ygb", [128, 2, 512], BF16)
        end3 = o[0]
        o[0] = base
        take(VB, "mqs0", [128, 4, 640], F32)
        take(VB, "mqs1", [128, 4, 640], F32)
        take(VB, "mts0", [128, 4, 5, 128], BF16)
        take(VB, "mts1", [128, 4, 5, 128], BF16)
        VB["mqs"] = [VB["mqs0"], VB["mqs1"]]
        VB["mts"] = [VB["mts0"], VB["mts1"]]
        assert max(end1, end2, end3, o[0]) <= self.uni_bytes, (end1, end2, end3, o[0])
        VC = {}
        o[0] = 0
        take(VC, "KaT", [128, 2, 2304], BF16)
        take(VC, "KcT", [128, 4, 2304], BF16)
        take(VC, "Va", [128, 18, 3, 66], BF16)
        take(VC, "Vc", [128, 18, 5, 66], BF16)
        take(VC, "qaug0", [128, 512], BF16)
        take(VC, "qaug1", [128, 512], BF16)
        VC["qaug"] = [VC["qaug0"], VC["qaug1"]]
        take(VC, "oaT", [64, 8, 512], BF16)
        take(VC, "ocT", [64, 4, 512], BF16)
        take(VC, "rope", [64, 2, 512], F32)
        take(VC, "mgen", [128, 4, 5, 128], BF16)
        take(VC, "mbrd0", [128, 4, 5, 128], BF16)
        take(VC, "mbrd1", [128, 4, 5, 128], BF16)
        VC["mbrd"] = [VC["mbrd0"], VC["mbrd1"]]
        take(VC, "merged", [128, 8, 512], BF16)
        take(VC, "actT", [128, 22, 512], BF16)
        assert o[0] <= self.uni_bytes, o[0]

        import os
        stop = os.environ.get("MK_STOP", "")

        class _Stop(Exception):
            pass

        def chk(tag):
            if stop == tag:
                raise _Stop()
        try:
            chk("setup")
            for l in range(DEPTH):
                last = l == DEPTH - 1
                self.R.pfx = "L%d." % l
                self.layer_params(l)
                chk("params%d" % l)
                self.build_masks(l, VB)
                chk("masks%d" % l)
                self.R.barrier()
                self.ssm_pre(l, VB)
                chk("pre%d" % l)
                self.phase_a1(l, VB)
                chk("a1_%d" % l)
                self.R.barrier()
                self.ssm_main_wrap(l, VB)
                chk("ssm%d" % l)
                self.R.barrier()
                self.init_stores(l, VC, True)
                chk("is%d" % l)
                self.dma(VC["mgen"][:].rearrange("p a b c -> p (a b c)"), self.mt[2], ["mt2"], ["mgen"], eng="pool", chan="msk")
                chk("mg%d" % l)
                for t in range(4):
                    self.R.pfx = "L%d.t%d." % (l, t)
                    self.phase_a2(l, t, VC)
                    chk("a2_%d_%d" % (l, t))
                chk("a2s%d" % l)
                for t in range(4):
                    self.R.pfx = "L%d.t%d." % (l, t)
                    self.phase_c(l, t, VC, last)
                    chk("c%d_%d" % (l, t))
                self.R.pfx = "L%d.t4." % l
                self.init_stores(l, VC, False)
                self.phase_a2(l, 4, VC)
                self.phase_c(l, 4, VC, last)
                chk("layer%d" % l)
                self.R.barrier()
        except _Stop:
            pass
        self.R.emit()
        import os as _os
        if _os.environ.get("MK_NAMES"):
            import json as _json
            _json.dump(self.R.names, open(_os.environ["MK_NAMES"], "w"))
        return nc

    def ssm_main_wrap(self, l, V):
        self.ssm_scan_only = True
        self.ssm_main(l, V)


_PROG_CACHE = {}
LAST_RESULTS = None


def _shapes(shared, percore):
    s = {}
    for k, v in shared.items():
        s[k] = v.shape
    for k, v in percore.items():
        s[k] = v.shape
    return s


def kernel(**inputs):
    inp = {k: np.asarray(v) for k, v in inputs.items()}
    shared = _prep_shared(inp)
    percore = [_prep_core(inp, c) for c in range(8)]
    shapes = _shapes(shared, percore[0])
    prog = Prog(shapes)
    nc = prog.build()
    in_maps = []
    for c in range(8):
        m = dict(shared)
        m.update(percore[c])
        in_maps.append({k: np.ascontiguousarray(v, dtype=np.float32) for k, v in m.items()})
    import os
    ncores = int(os.environ.get("MK_CORES", "8"))
    res = run_bass_kernel_spmd(nc, in_maps[:ncores], core_ids=list(range(ncores)))
    R = res.results
    global LAST_RESULTS
    LAST_RESULTS = R
    B, S = 16, 256
    y_prompt = np.zeros((B, S, D), np.float32)
    y_sample = np.zeros((4, LS, D), np.float32)
    new_ga_k = np.zeros((B, DEPTH, S, 2, 64), np.float32)
    new_ga_v = np.zeros((B, DEPTH, S, 2, 64), np.float32)
    new_na_k = np.zeros((B, DEPTH, S, 4, 64), np.float32)
    new_na_v = np.zeros((B, DEPTH, S, 4, 64), np.float32)
    new_ssm = np.zeros((B, DEPTH, 2, 2, 16, 64), np.float32)
    for c in range(ncores):
        r = R[c]
        yo = np.asarray(r["y_out"]).reshape(NT, 128, KC, TT)
        ytm = yo.transpose(0, 3, 2, 1).reshape(NT, TT, D)
        if c % 2 == 0:
            y_sample[c // 2] = ytm[0:4].reshape(LS, D)
        y_prompt[2 * c] = ytm[4, 0:256]
        y_prompt[2 * c + 1] = ytm[4, 256:512]
        nka = np.asarray(r["nk_a"]).reshape(DEPTH, 64, 2, 2, 256)
        nva = np.asarray(r["nv_a"]).reshape(DEPTH, 128, 2, 2, 2, 64)
        nkc = np.asarray(r["nk_c"]).reshape(DEPTH, 64, 4, 2, 256)
        nvc = np.asarray(r["nv_c"]).reshape(DEPTH, 128, 2, 2, 4, 64)
        nss = np.asarray(r["nssm"]).reshape(DEPTH, 2, 64, 2, 8, 2, 2)
        for s_ in range(2):
            bb = 2 * c + s_
            new_ga_k[bb] = nka[:, :, :, s_, :].transpose(0, 3, 2, 1)
            new_na_k[bb] = nkc[:, :, :, s_, :].transpose(0, 3, 2, 1)
            new_ga_v[bb] = nva[:, :, s_].transpose(0, 2, 1, 3, 4).reshape(DEPTH, 256, 2, 64)
            new_na_v[bb] = nvc[:, :, s_].transpose(0, 2, 1, 3, 4).reshape(DEPTH, 256, 4, 64)
            x = nss[:, :, :, :, :, s_, :]
            new_ssm[bb] = x.transpose(0, 3, 5, 4, 1, 2).reshape(DEPTH, 2, 2, 16, 64)
    return (y_prompt, y_sample, new_ga_k, new_ga_v, new_na_k, new_na_v, new_ssm)
```
